# Optimizing a Trainium2 kernel written in Bass

```python
import math
import jax, jax.numpy as jnp
from jax import lax
import numpy as np

D_MODEL = 1024
BATCH = 16
SEQ = 2048
DEPTH = 4
DEC_BATCH = 16
DEC_SEQ = 4096
PAST_LEN = 128

N_MIXERS = 2
N_A_LAYERS = (DEPTH + 1) // 2
N_B_LAYERS = DEPTH // 2
HEAD_DIM = 64
ROPE_THETA = 10000.0
RMS_EPS = 1e-6
BLOCK = 128
A_HEADS = D_MODEL // HEAD_DIM
A_KV_HEADS = A_HEADS // 4
A_GROUP = A_HEADS // A_KV_HEADS
A_Q_DIM = A_HEADS * HEAD_DIM
A_KV_DIM = A_KV_HEADS * HEAD_DIM
A_IN_DIM = A_Q_DIM + 2 * A_KV_DIM
WINDOW = 128
A_SPAN = BLOCK + 2 * WINDOW
B_HEADS = D_MODEL // (2 * HEAD_DIM)
B_QK_DIM = B_HEADS * 2 * HEAD_DIM
B_V_DIM = B_HEADS * 2 * HEAD_DIM
B_IN_DIM = 2 * B_QK_DIM + B_V_DIM
D_FF = int(math.ceil(8 * D_MODEL / 3 / 256) * 256)

kernel_name = "hybrid_swa_sink_diffattn_encoder"


def rms_norm(x, g):
    xf = x.astype(jnp.float32)
    y = xf * lax.rsqrt(jnp.mean(xf * xf, axis=-1, keepdims=True) + RMS_EPS)
    return (y * g.astype(jnp.float32)).astype(x.dtype)


def rope_tables(seq, dim):
    inv = 1.0 / (ROPE_THETA ** (jnp.arange(0, dim, 2, dtype=jnp.float32) / dim))
    ang = jnp.arange(seq, dtype=jnp.float32)[:, None] * inv[None, :]
    return jnp.cos(ang), jnp.sin(ang)


def apply_rope(x, cos, sin):
    x1, x2 = jnp.split(x, 2, axis=-1)
    c = cos[None, :, None, :].astype(x.dtype)
    s = sin[None, :, None, :].astype(x.dtype)
    return jnp.concatenate([x1 * c - x2 * s, x2 * c + x1 * s], axis=-1)


def window_gqa_sink(h, w_in, w_o, sinks):
    b, s, _ = h.shape
    n_blk = s // BLOCK
    qkv = h @ w_in
    q, k, v = jnp.split(qkv, [A_Q_DIM, A_Q_DIM + A_KV_DIM], axis=-1)
    q = q.reshape(b, s, A_HEADS, HEAD_DIM)
    k = k.reshape(b, s, A_KV_HEADS, HEAD_DIM)
    v = v.reshape(b, s, A_KV_HEADS, HEAD_DIM)
    cos, sin = rope_tables(s, HEAD_DIM)
    q = apply_rope(q, cos, sin) * (HEAD_DIM ** -0.5)
    k = apply_rope(k, cos, sin)
    q = q.reshape(b, n_blk, BLOCK, A_KV_HEADS, A_GROUP, HEAD_DIM).transpose(1, 0, 2, 3, 4, 5)
    pad = ((0, 0), (WINDOW, WINDOW), (0, 0), (0, 0))
    kp = jnp.pad(k, pad)
    vp = jnp.pad(v, pad)
    sink = sinks.astype(jnp.float32).reshape(1, A_KV_HEADS, A_GROUP, 1, 1)

    def block(args):
        qi, i = args
        start = i * BLOCK
        kb = lax.dynamic_slice_in_dim(kp, start, A_SPAN, axis=1)
        vb = lax.dynamic_slice_in_dim(vp, start, A_SPAN, axis=1)
        sc = jnp.einsum('bqkgd,bskd->bkgqs', qi, kb).astype(jnp.float32)
        qpos = start + jnp.arange(BLOCK)
        kpos = start - WINDOW + jnp.arange(A_SPAN)
        valid = (jnp.abs(qpos[:, None] - kpos[None, :]) <= WINDOW) & (kpos >= 0)[None, :] & (kpos < s)[None, :]
        sc = jnp.where(valid, sc, -jnp.inf)
        m = jnp.maximum(jnp.max(sc, axis=-1, keepdims=True), sink)
        e = jnp.exp(sc - m)
        p = e / (jnp.sum(e, axis=-1, keepdims=True) + jnp.exp(sink - m))
        return jnp.einsum('bkgqs,bskd->bqkgd', p.astype(vb.dtype), vb)

    o = lax.map(block, (q, jnp.arange(n_blk)))
    o = o.transpose(1, 0, 2, 3, 4, 5).reshape(b, s, A_Q_DIM)
    return o @ w_o


def diff_attention(h, w_in, w_o, lq1, lk1, lq2, lk2, subln_g, lambda_init):
    b, s, _ = h.shape
    n_blk = s // BLOCK
    qkv = h @ w_in
    q, k, v = jnp.split(qkv, [B_QK_DIM, 2 * B_QK_DIM], axis=-1)
    q = q.reshape(b, s, 2 * B_HEADS, HEAD_DIM)
    k = k.reshape(b, s, 2 * B_HEADS, HEAD_DIM)
    v = v.reshape(b, s, B_HEADS, 2 * HEAD_DIM)
    cos, sin = rope_tables(s, HEAD_DIM)
    q = (apply_rope(q, cos, sin) * (HEAD_DIM ** -0.5)).reshape(b, s, B_HEADS, 2, HEAD_DIM)
    k = apply_rope(k, cos, sin).reshape(b, s, B_HEADS, 2, HEAD_DIM)
    lam = (jnp.exp(jnp.sum(lq1.astype(jnp.float32) * lk1.astype(jnp.float32)))
           - jnp.exp(jnp.sum(lq2.astype(jnp.float32) * lk2.astype(jnp.float32)))
           + lambda_init)
    qb = q.reshape(b, n_blk, BLOCK, B_HEADS, 2, HEAD_DIM).transpose(1, 0, 2, 3, 4, 5)

    def block(qi):
        sc = jnp.einsum('bqhcd,bkhcd->bhcqk', qi, k).astype(jnp.float32)
        p = jax.nn.softmax(sc, axis=-1)
        a = p[:, :, 0] - lam * p[:, :, 1]
        return jnp.einsum('bhqk,bkhe->bqhe', a.astype(v.dtype), v)

    o = lax.map(block, qb)
    o = o.transpose(1, 0, 2, 3, 4).reshape(b, s, B_HEADS, 2 * HEAD_DIM)
    o = rms_norm(o, subln_g) * (1.0 - lambda_init)
    return o.reshape(b, s, B_V_DIM) @ w_o


def swiglu(h, w_gate_up, w_down):
    g, u = jnp.split(h @ w_gate_up, 2, axis=-1)
    return (jax.nn.silu(g) * u) @ w_down


def trunk(x, norm_mix_pre, norm_mix_post, norm_ffn_pre, norm_ffn_post,
          a_w_in, a_w_o, a_sinks, b_w_in, b_w_o, b_lambda_q1, b_lambda_k1,
          b_lambda_q2, b_lambda_k2, b_subln, ffn_w_gate_up, ffn_w_down):
    for i in range(DEPTH):
        h = rms_norm(x, norm_mix_pre[i])
        j = i // N_MIXERS
        if i % N_MIXERS == 0:
            m = window_gqa_sink(h, a_w_in[j], a_w_o[j], a_sinks[j])
        else:
            lambda_init = 0.8 - 0.6 * math.exp(-0.3 * i)
            m = diff_attention(h, b_w_in[j], b_w_o[j], b_lambda_q1[j], b_lambda_k1[j],
                               b_lambda_q2[j], b_lambda_k2[j], b_subln[j], lambda_init)
        x = x + rms_norm(m, norm_mix_post[i])
        h = rms_norm(x, norm_ffn_pre[i])
        x = x + rms_norm(swiglu(h, ffn_w_gate_up[i], ffn_w_down[i]), norm_ffn_post[i])
    return x


def setup_inputs(seed: int = 0) -> dict:
    key = jax.random.key(seed)
    ks = jax.random.split(key, 20)
    f32 = jnp.float32

    def w(k, shape, fan_in):
        return jax.random.normal(k, shape, f32) * (fan_in ** -0.5)

    def gain(k, shape):
        return 1.0 + 0.05 * jax.random.normal(k, shape, f32)

    return {
        "x_prompt": jax.random.normal(ks[0], (BATCH, SEQ, D_MODEL), f32),
        "x_sample": jax.random.normal(ks[1], (DEC_BATCH, DEC_SEQ, D_MODEL), f32),
        "norm_mix_pre": gain(ks[2], (DEPTH, D_MODEL)),
        "norm_mix_post": gain(ks[3], (DEPTH, D_MODEL)),
        "norm_ffn_pre": gain(ks[4], (DEPTH, D_MODEL)),
        "norm_ffn_post": gain(ks[5], (DEPTH, D_MODEL)),
        "a_w_in": w(ks[6], (N_A_LAYERS, D_MODEL, A_IN_DIM), D_MODEL),
        "a_w_o": w(ks[7], (N_A_LAYERS, A_Q_DIM, D_MODEL), A_Q_DIM),
        "a_sinks": jax.random.normal(ks[8], (N_A_LAYERS, A_HEADS), f32),
        "b_w_in": w(ks[9], (N_B_LAYERS, D_MODEL, B_IN_DIM), D_MODEL),
        "b_w_o": w(ks[10], (N_B_LAYERS, B_V_DIM, D_MODEL), B_V_DIM),
        "b_lambda_q1": 0.1 * jax.random.normal(ks[11], (N_B_LAYERS, HEAD_DIM), f32),
        "b_lambda_k1": 0.1 * jax.random.normal(ks[12], (N_B_LAYERS, HEAD_DIM), f32),
        "b_lambda_q2": 0.1 * jax.random.normal(ks[13], (N_B_LAYERS, HEAD_DIM), f32),
        "b_lambda_k2": 0.1 * jax.random.normal(ks[14], (N_B_LAYERS, HEAD_DIM), f32),
        "b_subln": gain(ks[15], (N_B_LAYERS, 2 * HEAD_DIM)),
        "ffn_w_gate_up": w(ks[16], (DEPTH, D_MODEL, 2 * D_FF), D_MODEL),
        "ffn_w_down": w(ks[17], (DEPTH, D_FF, D_MODEL), D_FF),
    }


def reference(x_prompt, x_sample, norm_mix_pre, norm_mix_post, norm_ffn_pre, norm_ffn_post,
              a_w_in, a_w_o, a_sinks, b_w_in, b_w_o, b_lambda_q1, b_lambda_k1,
              b_lambda_q2, b_lambda_k2, b_subln, ffn_w_gate_up, ffn_w_down):
    y_prompt = trunk(x_prompt, norm_mix_pre, norm_mix_post, norm_ffn_pre, norm_ffn_post,
                     a_w_in, a_w_o, a_sinks, b_w_in, b_w_o, b_lambda_q1, b_lambda_k1,
                     b_lambda_q2, b_lambda_k2, b_subln, ffn_w_gate_up, ffn_w_down)
    y_sample = trunk(x_sample, norm_mix_pre, norm_mix_post, norm_ffn_pre, norm_ffn_post,
                     a_w_in, a_w_o, a_sinks, b_w_in, b_w_o, b_lambda_q1, b_lambda_k1,
                     b_lambda_q2, b_lambda_k2, b_subln, ffn_w_gate_up, ffn_w_down)
    return (y_prompt, y_sample)
```

```python
import math
import numpy as np
import concourse.bass as bass
import concourse.mybir as mybir
from concourse.bass_utils import run_bass_kernel_spmd

F32 = mybir.dt.float32
BF16 = mybir.dt.bfloat16
AF = mybir.ActivationFunctionType
ALU = mybir.AluOpType
AX = mybir.AxisListType

D = 1024
KC = 8
DFF = 2816
NFF = 22
EPS = 1e-6
NDMA_SEM = 40
import os
DBG = int(os.environ.get('KDBG', '9'))
KV = int(os.environ.get('KV', '0'))
ENGS = ("pe", "act", "dve", "pool", "sp")


class Buf:
    __slots__ = ("name", "w", "r", "region")

    def __init__(self, name, region=False):
        self.name = name
        self.w = {}
        self.r = {}
        self.region = region


class Op:
    __slots__ = ("eng", "fn", "deps", "flag", "cnt", "dma", "dsem", "dval", "xb")


class Sched:
    def __init__(self):
        self.q = {e: [] for e in ENGS}
        self.ndma = {e: 0 for e in ENGS}
        self.dma_hist = {e: [] for e in ENGS}
        self.reg_eng = {}
        self.reg_dma = {}
        self.barrier = []
        self.barrier_done = set(ENGS)
        self.nops = 0

    @staticmethod
    def _norm(items):
        out = []
        for it in items:
            if isinstance(it, tuple):
                out.append(it)
            else:
                out.append((it, None))
        return out

    def phase_switch(self):
        self.barrier = list(self.reg_eng.values()) + list(self.reg_dma.values())
        self.barrier_done = set()

    def add(self, eng, fn, R=(), W=(), X=(), dma=False):
        op = Op()
        op.eng, op.fn, op.dma, op.flag, op.cnt = eng, fn, dma, False, 0
        op.dsem = op.dval = None
        deps = []
        R = self._norm(R)
        X = self._norm(X)
        op.xb = set(id(b) for b, _ in X)
        W = self._norm(W) + X
        touches_region = False
        for b, t in R:
            touches_region |= b.region
            if t is None:
                for w in b.w.values():
                    deps.append((w, True))
            else:
                w = b.w.get(t)
                if w is not None:
                    deps.append((w, True))
                w = b.w.get(None)
                if w is not None:
                    deps.append((w, True))
        for b, t in W:
            touches_region |= b.region
            isx = id(b) in op.xb
            if t is None:
                for w in b.w.values():
                    deps.append((w, 2 if (isx and id(b) in w.xb) else False))
                for rd in b.r.values():
                    for r in rd.values():
                        deps.append((r, False))
            else:
                for tt in (t, None):
                    w = b.w.get(tt)
                    if w is not None:
                        deps.append((w, 2 if (isx and id(b) in w.xb) else False))
                    rd = b.r.get(tt)
                    if rd:
                        for r in rd.values():
                            deps.append((r, False))
        if touches_region and eng not in self.barrier_done:
            self.barrier_done.add(eng)
            for d in self.barrier:
                deps.append((d, False))
        if dma:
            k = self.ndma[eng]
            op.dsem = (eng, k % NDMA_SEM)
        rkey = op.dsem if dma else eng
        for b, t in R:
            b.r.setdefault(t, {})[rkey] = op
        for b, t in W:
            if t is None:
                b.w = {None: op}
                b.r = {}
            else:
                b.w[t] = op
                b.r[t] = {}
        if dma:
            k = self.ndma[eng]
            self.ndma[eng] = k + 1
            op.dsem = (eng, k % NDMA_SEM)
            op.dval = 16 * (k // NDMA_SEM + 1)
            hist = self.dma_hist[eng]
            if k >= NDMA_SEM:
                deps.append((hist[k - NDMA_SEM], False))
            hist.append(op)
            if touches_region:
                self.reg_dma[op.dsem] = op
        elif touches_region:
            self.reg_eng[eng] = op
        op.deps = deps
        self.q[eng].append(op)
        self.nops += 1
        return op

    def finalize(self):
        for e in ENGS:
            for op in self.q[e]:
                for d, raw in op.deps:
                    if d.dma:
                        continue
                    if d.eng == e and (e == "pe" or raw == 2):
                        continue
                    d.flag = True
        for e in ENGS:
            c = 0
            for op in self.q[e]:
                if op.dma:
                    continue
                if op.flag:
                    c += 1
                    op.cnt = c

    def emit(self, eng, handle, sems):
        waited = {}
        nw = 0
        for op in self.q[eng]:
            for d, raw in op.deps:
                if d.dma:
                    key, val = d.dsem, d.dval
                else:
                    if d.eng == eng and (eng == "pe" or raw == 2):
                        continue
                    key, val = d.eng, d.cnt
                if waited.get(key, 0) >= val:
                    continue
                handle.wait_ge(sems[key], val)
                waited[key] = val
                nw += 1
            if op.fn is None:
                continue
            inst = op.fn(handle)
            if op.dma:
                inst.then_inc(sems[op.dsem], 16)
            elif op.flag:
                inst.then_inc(sems[eng], 1)
        return nw


class SB(Buf):
    __slots__ = ("ap", "nbytes")

    def __init__(self, name, ap, nbytes, region=False):
        Buf.__init__(self, name, region)
        self.ap = ap
        self.nbytes = nbytes


class Arena:
    def __init__(self, nc, nbytes):
        self.h = nc.alloc_sbuf_tensor("arena", [128, nbytes // 2], BF16)
        self.ap = self.h.ap()
        self.nbytes = nbytes
        self.shared_top = 0
        self.reg_top = 0

    def _view(self, off, shape, dt):
        n = 1
        for s in shape:
            n *= s
        esz = 4 if dt == F32 else 2
        nb = n * esz
        nb_al = (nb + 31) // 32 * 32
        assert off + nb_al <= self.nbytes, ("SBUF arena overflow", off, nb_al, self.nbytes)
        a = self.ap[:, off // 2: off // 2 + nb // 2]
        if dt == F32:
            a = a.bitcast(F32)
        if len(shape) == 2:
            a = a.rearrange("p (a b) -> p a b", a=shape[0])
        elif len(shape) == 3:
            a = a.rearrange("p (a b c) -> p a b c", a=shape[0], b=shape[1])
        elif len(shape) == 4:
            a = a.rearrange("p (a b c d) -> p a b c d", a=shape[0], b=shape[1], c=shape[2])
        return a, nb_al

    def shared(self, name, shape, dt):
        a, nb = self._view(self.shared_top, shape, dt)
        self.shared_top += nb
        self.reg_top = self.shared_top
        return SB(name, a, nb, False)

    def reset_region(self):
        self.reg_top = self.shared_top

    def reg(self, name, shape, dt):
        a, nb = self._view(self.reg_top, shape, dt)
        self.reg_top += nb
        return SB(name, a, nb, True)


def lambda_init_of(layer_idx):
    return 0.8 - 0.6 * math.exp(-0.3 * layer_idx)


def build(cfg):
    DEPTH = cfg["depth"]
    NA = (DEPTH + 1) // 2
    NB = DEPTH // 2
    seqdefs = cfg["seqs"]
    SMAX = max(s for s, _ in seqdefs)
    NTMAX = SMAX // 128

    nc = bass.Bass("TRN2", target_bir_lowering=False)
    S = Sched()

    def din(name, shape):
        return nc.dram_tensor(name, list(shape), F32, kind="ExternalInput").ap()

    xin = [din(f"x{i}", (n, s, D)) for i, (s, n) in enumerate(seqdefs)]
    yout = [nc.dram_tensor(f"y{i}", [n, s, D], F32, kind="ExternalOutput").ap()
            for i, (s, n) in enumerate(seqdefs)]
    g_mix_pre = din("norm_mix_pre", (DEPTH, D))
    g_mix_post = din("norm_mix_post", (DEPTH, D))
    g_ffn_pre = din("norm_ffn_pre", (DEPTH, D))
    g_ffn_post = din("norm_ffn_post", (DEPTH, D))
    a_w_in = din("a_w_in", (NA, D, 1536))
    a_w_o = din("a_w_o", (NA, D, D))
    a_sinks = din("a_sinks", (NA, 16))
    if NB:
        b_w_in = din("b_w_in", (NB, D, 3072))
        b_w_o = din("b_w_o", (NB, D, D))
        b_lq1 = din("b_lambda_q1", (NB, 64))
        b_lk1 = din("b_lambda_k1", (NB, 64))
        b_lq2 = din("b_lambda_q2", (NB, 64))
        b_lk2 = din("b_lambda_k2", (NB, 64))
        b_subln = din("b_subln", (NB, 128))
    w_gu = din("ffn_w_gate_up", (DEPTH, D, 2 * DFF))
    w_dn = din("ffn_w_down", (DEPTH, DFF, D))
    cosT = din("cos_t", (128, SMAX))
    sinT = din("sin_t", (128, SMAX))
    cst = din("cst", (128, 4, 128))

    def scr(name, shape):
        return nc.dram_tensor(name, list(shape), BF16, kind="Internal").ap()

    WINA = [scr(f"wina{l}", (7, 128, KC, 256)) for l in range(NA)]
    WOA = [scr(f"woa{l}", (D, D)) for l in range(NA)]
    WINB = [scr(f"winb{l}", (12, 128, KC, 256)) for l in range(NB)]
    WOB = [scr(f"wob{l}", (D, D)) for l in range(NB)]
    WGU = [scr(f"wgu{i}", (NFF, 128, KC, 256)) for i in range(DEPTH)]
    WDN = [scr(f"wdn{i}", (DFF, D)) for i in range(DEPTH)]
    QT = scr("qt", (8, 128, SMAX))
    KT = scr("kt", (8, 128, SMAX))
    VB = scr("vb", (8, 128, NTMAX, 129))
    VA = scr("va", (128, NTMAX, 4, 65))

    b_WINA = [Buf(f"wina{l}") for l in range(NA)]
    b_WOA = [Buf(f"woa{l}") for l in range(NA)]
    b_WINB = [Buf(f"winb{l}") for l in range(NB)]
    b_WOB = [Buf(f"wob{l}") for l in range(NB)]
    b_WGU = [Buf(f"wgu{i}") for i in range(DEPTH)]
    b_WDN = [Buf(f"wdn{i}") for i in range(DEPTH)]
    b_QT = [Buf(f"qt{c}") for c in range(8)]
    b_KT = [Buf(f"kt{c}") for c in range(8)]
    b_VB = Buf("vb")
    b_VA = Buf("va")
    b_IN = Buf("inputs")

    seqs = []
    for i, (s, n) in enumerate(seqdefs):
        for k in range(n):
            seqs.append(dict(S=s, xin=xin[i][k], y=yout[i][k],
                             yb=[Buf(f"y{i}_{k}_{t}") for t in range(s // 128)]))

    A = Arena(nc, cfg.get("arena", 204 * 1024))
    ps = nc.alloc_psum_tensor("ps", [128, 8, 512], F32).ap()
    PB = [Buf(f"psum{b}") for b in range(8)]

    def trv(b):
        return ps[:, b, :].bitcast(BF16).rearrange("p (k t) -> p k t", k=8)

    CSTB = A.shared("cstb", [4, 128], BF16)
    IDB = CSTB.ap[:, 0, :]
    RMB = CSTB.ap[:, 1, :]
    MKP = CSTB.ap[:, 2, :]
    MKN = CSTB.ap[:, 3, :]
    GPRE = A.shared("gpre", [2, DEPTH, 8], F32)
    ESK = A.shared("esk", [NA, 4, 4], F32)
    NEGLAM = A.shared("neglam", [max(NB, 1)], F32)
    SUBG = A.shared("subg", [max(NB, 1), 128], F32)
    EPSB = A.shared("epsb", [8], F32)
    GPM = A.shared("gpm", [D], F32)
    GPF = A.shared("gpf", [D], F32)
    XT = [A.shared(f"xt{i}", [D], F32) for i in range(3)]
    XN = [A.shared(f"xn{i}", [D], BF16) for i in range(2)]
    JK = [A.shared(f"junk{i}", [D], BF16) for i in range(3)]
    HT = [A.shared(f"ht{i}", [KC, 512], BF16) for i in range(2)]
    STAT = [A.shared(f"stat{i}", [8], F32) for i in range(16)]
    WS = [A.shared(f"ws{i}", [KC, 256], BF16) for i in range(4)]
    WO = A.shared("wo", [KC, D], BF16)
    TMP = [A.shared(f"tmp{i}", [D], F32) for i in range(2)]

    ctr = {}

    def nxt(k):
        v = ctr.get(k, 0)
        ctr[k] = v + 1
        return v

    def dma(q, out, in_, R, W, **kw):
        return S.add(q, lambda e, o=out, i=in_, kw=kw: e.dma_start(out=o, in_=i, **kw), R=R, W=W, dma=True)

    def layer_kind(l):
        return ("A", l // 2) if l % 2 == 0 else ("B", l // 2)

    def prep():
        A.reset_region()
        S.phase_switch()
        F = [A.reg(f"pf{i}", [3072], F32) for i in range(2)]
        Bq = [A.reg(f"pb{i}", [3072], BF16) for i in range(2)]
        SM = A.reg("psm", [1024], F32)
        dma("sp", F[0].ap[:, 0:512], cst.rearrange("p a b -> p (a b)"), [b_IN], [F[0]])
        S.add("dve", lambda e: e.tensor_copy(out=CSTB.ap.rearrange("p a b -> p (a b)"), in_=F[0].ap[:, 0:512]),
              R=[F[0]], W=[CSTB])
        S.add("dve", lambda e: e.memset(EPSB.ap, EPS), W=[EPSB])
        dma("sp", GPRE.ap[:, 0, :, :], g_mix_pre.rearrange("l (kc p) -> p l kc", p=128), [b_IN], [(GPRE, 0)],
            allow_slow_non_contiguous=True)
        dma("sp", GPRE.ap[:, 1, :, :], g_ffn_pre.rearrange("l (kc p) -> p l kc", p=128), [b_IN], [(GPRE, 1)],
            allow_slow_non_contiguous=True)
        for la in range(NA):
            src = a_sinks[la:la + 1, :].partition_broadcast(128)[:, 0, :]
            dma("sp", SM.ap[:, la * 16:(la + 1) * 16], src, [b_IN], [(SM, la)])
        S.add("act", lambda e: e.activation(out=ESK.ap.rearrange("p a b c -> p (a b c)"), in_=SM.ap[:, 0:NA * 16],
                                            func=AF.Exp), R=[SM], W=[ESK])
        if NB:
            o = 64
            for nm, src in (("q1", b_lq1), ("k1", b_lk1), ("q2", b_lq2), ("k2", b_lk2)):
                s2 = src.rearrange("(o a) b -> o (a b)", o=1).partition_broadcast(128)[:, 0, :]
                dma("sp", SM.ap[:, o:o + NB * 64], s2, [b_IN], [(SM, nm)])
                o += NB * 64
            sg = b_subln.rearrange("(o a) b -> o (a b)", o=1).partition_broadcast(128)[:, 0, :]
            dma("sp", SM.ap[:, o:o + NB * 128], sg, [b_IN], [(SM, "sg")])
            osg = o
            q1 = SM.ap[:, 64:64 + NB * 64]
            k1 = SM.ap[:, 64 + NB * 64:64 + 2 * NB * 64]
            q2 = SM.ap[:, 64 + 2 * NB * 64:64 + 3 * NB * 64]
            k2 = SM.ap[:, 64 + 3 * NB * 64:64 + 4 * NB * 64]
            pr = A.reg("ppr", [2, NB, 64], F32)
            rd = A.reg("prd", [2, NB], F32)
            ex = A.reg("pex", [2, NB], F32)
            S.add("dve", lambda e: e.tensor_tensor(out=pr.ap[:, 0].rearrange("p a b -> p (a b)"), in0=q1, in1=k1,
                                                   op=ALU.mult), R=[SM], W=[(pr, 0)])
            S.add("dve", lambda e: e.tensor_tensor(out=pr.ap[:, 1].rearrange("p a b -> p (a b)"), in0=q2, in1=k2,
                                                   op=ALU.mult), R=[SM], W=[(pr, 1)])
            S.add("dve", lambda e: e.tensor_reduce(out=rd.ap.rearrange("p a b -> p (a b)"),
                                                   in_=pr.ap.rearrange("p a b c -> p (a b) c"),
                                                   axis=AX.X, op=ALU.add), R=[pr], W=[rd])
            S.add("act", lambda e: e.activation(out=ex.ap.rearrange("p a b -> p (a b)"),
                                                in_=rd.ap.rearrange("p a b -> p (a b)"), func=AF.Exp),
                  R=[rd], W=[ex])
            S.add("dve", lambda e: e.tensor_tensor(out=rd.ap[:, 0, :], in0=ex.ap[:, 1, :], in1=ex.ap[:, 0, :],
                                                   op=ALU.subtract), R=[ex], W=[rd])
            for j in range(NB):
                li = lambda_init_of(2 * j + 1)
                S.add("dve", lambda e, j=j, li=li: e.tensor_scalar_add(out=NEGLAM.ap[:, j:j + 1],
                                                                       in0=rd.ap[:, 0, j:j + 1], scalar1=-li),
                      R=[rd], W=[(NEGLAM, j)])
                S.add("dve", lambda e, j=j, li=li: e.tensor_scalar_mul(
                    out=SUBG.ap[:, j, :], in0=SM.ap[:, osg + j * 128: osg + (j + 1) * 128], scalar1=(1.0 - li)),
                    R=[SM], W=[(SUBG, j)])

        it = [0]

        def conv(src_ap, ncols, stores, wbuf):
            i = it[0]
            it[0] += 1
            f, b = F[i % 2], Bq[i % 2]
            dma("sp", f.ap[:, 0:ncols], src_ap, [b_IN], [f])
            if i % 2 == 0:
                S.add("dve", lambda e: e.tensor_copy(out=b.ap[:, 0:ncols], in_=f.ap[:, 0:ncols]), R=[f], W=[b])
            else:
                S.add("act", lambda e: e.activation(out=b.ap[:, 0:ncols], in_=f.ap[:, 0:ncols], func=AF.Copy),
                      R=[f], W=[b])
            for dst, srcv in stores(b.ap):
                dma("pool", dst, srcv, [b], [(wbuf, nxt("wtag"))])

        for l in range(NA):
            for kc in range(KC):
                def st(bap, l=l, kc=kc):
                    out = []
                    out.append((WINA[l][0:4, :, kc, :].rearrange("s p c -> p s c"),
                                bap[:, 0:1024].rearrange("p (s c) -> p s c", s=4)))
                    out.append((WINA[l][6, :, kc, :], bap[:, 1280:1536]))
                    for s_ in range(2):
                        for d in range(2):
                            dst = WINA[l][4 + s_, :, kc, :].rearrange("p (k d c) -> p k d c", k=2, d=2)[:, :, d, :]
                            out.append((dst, bap[:, 1024 + s_ * 128:1152 + s_ * 128].rearrange(
                                "p (k c) -> p k c", k=2)))
                    return out
                conv(a_w_in[l, kc * 128:(kc + 1) * 128, :], 1536, st, b_WINA[l])
            for kc in range(KC):
                conv(a_w_o[l, kc * 128:(kc + 1) * 128, :], D,
                     lambda bap, l=l, kc=kc: [(WOA[l][kc * 128:(kc + 1) * 128, :], bap[:, 0:D])], b_WOA[l])
        for l in range(NB):
            for kc in range(KC):
                conv(b_w_in[l, kc * 128:(kc + 1) * 128, :], 3072,
                     lambda bap, l=l, kc=kc: [(WINB[l][:, :, kc, :].rearrange("s p c -> p s c"),
                                               bap[:, 0:3072].rearrange("p (s c) -> p s c", s=12))], b_WINB[l])
            for kc in range(KC):
                conv(b_w_o[l, kc * 128:(kc + 1) * 128, :], D,
                     lambda bap, l=l, kc=kc: [(WOB[l][kc * 128:(kc + 1) * 128, :], bap[:, 0:D])], b_WOB[l])
        for i in range(DEPTH):
            for kc in range(KC):
                for half in range(2):
                    conv(w_gu[i, kc * 128:(kc + 1) * 128, half * DFF:(half + 1) * DFF], DFF,
                         lambda bap, i=i, kc=kc, half=half: [
                             (WGU[i][:, :, kc, half * 128:(half + 1) * 128].rearrange("j p c -> p j c"),
                              bap[:, 0:DFF].rearrange("p (j c) -> p j c", j=NFF))], b_WGU[i])
            for rc in range(NFF):
                conv(w_dn[i, rc * 128:(rc + 1) * 128, :], D,
                     lambda bap, i=i, rc=rc: [(WDN[i][rc * 128:(rc + 1) * 128, :], bap[:, 0:D])], b_WDN[i])

    def xsrc(sq, l, tile, mid):
        if l == 0 and not mid:
            return sq["xin"][tile * 128:(tile + 1) * 128, :], b_IN
        return sq["y"][tile * 128:(tile + 1) * 128, :], sq["yb"][tile]

    def rstd_ops(st, n, dim):
        S.add("act", lambda e: e.activation(out=st.ap[:, n:2 * n], in_=st.ap[:, 0:n], func=AF.Ln,
                                            scale=1.0 / dim, bias=EPSB.ap[:, 0:1]),
              R=[(st, 0), EPSB], W=[(st, 1)])
        S.add("act", lambda e: e.activation(out=st.ap[:, 2 * n:3 * n], in_=st.ap[:, n:2 * n], func=AF.Exp,
                                            scale=-0.5), R=[(st, 1)], W=[(st, 2)])

    def front(sq, l, which, g, mid):
        ht = HT[g % 2]
        for t in range(4):
            tile = 4 * g + t
            xt = XT[nxt("xt") % 3]
            st = STAT[nxt("st") % 16]
            xn = XN[nxt("xn") % 2]
            tb = nxt("tr") % 2
            src, sb = xsrc(sq, l, tile, mid)
            dma("sp", xt.ap, src, [sb], [xt])
            jk = JK[nxt("jk") % 3]
            S.add("act", lambda e, xt=xt, st=st, jk=jk: e.activation(out=jk.ap, in_=xt.ap, func=AF.Square,
                                                                     accum_out=st.ap[:, 0:1]),
                  R=[xt], W=[(st, 0), jk])
            rstd_ops(st, 1, D)
            S.add("dve", lambda e, xt=xt, st=st, xn=xn: e.tensor_scalar(
                out=xn.ap, in0=xt.ap, scalar1=st.ap[:, 2:3], scalar2=None, op0=ALU.mult),
                R=[xt, (st, 2)], W=[xn])

            def tr(e, xn=xn, tb=tb):
                for kc in range(KC):
                    i = e.transpose(trv(tb)[:, kc, :], xn.ap[:, kc * 128:(kc + 1) * 128], IDB)
                return i
            S.add("pe", tr, R=[xn, CSTB], W=[PB[tb]])
            S.add("dve", lambda e, ht=ht, t=t, tb=tb: e.tensor_tensor(
                out=ht.ap[:, :, t * 128:(t + 1) * 128], in0=trv(tb),
                in1=GPRE.ap[:, which, l, :].unsqueeze(2).to_broadcast([128, KC, 128]), op=ALU.mult),
                R=[GPRE], W=[(ht, t)], X=[PB[tb]])

    def P1(sq, l):
        kind, j = layer_kind(l)
        Sq = sq["S"]
        NG = Sq // 512
        A.reset_region()
        S.phase_switch()
        TC = [A.reg(f"tc{i}", [512], F32) for i in range(2)]
        TS = [A.reg(f"ts{i}", [512], F32) for i in range(2)]
        QR = [A.reg(f"qr{i}", [512], BF16) for i in range(3)]
        T1 = [A.reg(f"t1{i}", [512], F32) for i in range(2)]
        T2 = [A.reg(f"t2{i}", [512], F32) for i in range(2)]
        QO = [A.reg(f"qo{i}", [512], BF16) for i in range(4)]
        if kind == "A":
            VST = A.reg("vst", [4, 4, 65], BF16)
            WIN, bWIN = WINA[j], b_WINA[j]
            slots = [("q", s) for s in range(4)] + [("k", s) for s in range(2)] + [("v", 0)]
        else:
            VST = A.reg("vst", [8, 4, 129], BF16)
            WIN, bWIN = WINB[j], b_WINB[j]
            slots = [("q", s) for s in range(4)] + [("k", s) for s in range(4)] + [("v", s) for s in range(4)]
        S.add("dve", lambda e: e.memset(VST.ap, 1.0), W=[VST])

        def proj(g):
            ht = HT[g % 2]
            tc, ts = TC[g % 2], TS[g % 2]
            dma("sp", tc.ap, cosT[:, g * 512:(g + 1) * 512], [b_IN], [tc])
            dma("sp", ts.ap, sinT[:, g * 512:(g + 1) * 512], [b_IN], [ts])
            pend = []

            def second(ci):
                (qr, pj, dstT, dbuf, c) = ci
                rb = 5 + nxt("rot") % 3
                S.add("pe", lambda e: e.matmul(ps[:, rb, :], lhsT=RMB, rhs=qr.ap, start=True, stop=True),
                      R=[qr, CSTB], W=[PB[rb]])
                t1 = T1[nxt("t1") % 2]
                t2 = T2[nxt("t2") % 2]
                qo = QO[nxt("qo") % 4]
                S.add("dve", lambda e: e.tensor_tensor(out=t1.ap, in0=ps[:, rb, :], in1=ts.ap, op=ALU.mult),
                      R=[ts], W=[t1], X=[PB[rb]])
                S.add("dve", lambda e: e.tensor_tensor(out=t2.ap, in0=qr.ap, in1=tc.ap, op=ALU.mult),
                      R=[qr, tc], W=[t2])
                S.add("dve", lambda e: e.tensor_tensor(out=qo.ap, in0=t1.ap, in1=t2.ap, op=ALU.add),
                      R=[t1, t2], W=[qo])
                dma("pool", dstT[c][:, g * 512:(g + 1) * 512], qo.ap, [qo], [(dbuf[c], g)])

            for si, (ty, s) in enumerate(slots):
                ws = WS[nxt("ws") % 4]
                dma("sp", ws.ap, WIN[si], [bWIN], [ws])
                if ty in ("q", "k"):
                    for half in range(2):
                        c = 2 * s + half
                        pj = 2 + nxt("pj") % 3
                        qr = QR[nxt("qr") % 3]

                        def mm(e, ws=ws, half=half, pj=pj):
                            for kc in range(KC):
                                i = e.matmul(ps[:, pj, :], lhsT=ws.ap[:, kc, half * 128:(half + 1) * 128],
                                             rhs=ht.ap[:, kc, :], start=(kc == 0), stop=(kc == KC - 1))
                            return i
                        S.add("pe", mm, R=[ws, ht], W=[PB[pj]])
                        S.add("act", lambda e, qr=qr, pj=pj: e.activation(out=qr.ap, in_=ps[:, pj, :], func=AF.Copy),
                              W=[qr], X=[PB[pj]])
                        if pend:
                            second(pend.pop(0))
                        pend.append((qr, pj, QT if ty == "q" else KT, b_QT if ty == "q" else b_KT, c))
                else:
                    while pend:
                        second(pend.pop(0))
                    for t in range(4):
                        pj = 2 + nxt("pj") % 3

                        def mmv(e, ws=ws, t=t, pj=pj):
                            for kc in range(KC):
                                i = e.matmul(ps[:, pj, 0:256], lhsT=ht.ap[:, kc, t * 128:(t + 1) * 128],
                                             rhs=ws.ap[:, kc, :], start=(kc == 0), stop=(kc == KC - 1))
                            return i
                        S.add("pe", mmv, R=[ws, ht], W=[PB[pj]])
                        if kind == "A":
                            o_ap = VST.ap[:, t, :, 0:64]
                            i_ap = ps[:, pj, 0:256].rearrange("p (k c) -> p k c", k=4)
                        else:
                            o_ap = VST.ap[:, 2 * s:2 * s + 2, t, 0:128]
                            i_ap = ps[:, pj, 0:256].rearrange("p (k c) -> p k c", k=2)
                        S.add("act", lambda e, o_ap=o_ap, i_ap=i_ap: e.activation(out=o_ap, in_=i_ap, func=AF.Copy),
                              W=[(VST, (t, s))], X=[PB[pj]])
            while pend:
                second(pend.pop(0))
            if kind == "A":
                dma("pool", VA[:, 4 * g:4 * g + 4, :, :], VST.ap, [VST], [(b_VA, g)])
            else:
                dma("pool", VB[:, :, 4 * g:4 * g + 4, :].rearrange("h p t c -> p h t c"), VST.ap, [VST], [(b_VB, g)])

        front(sq, l, 0, 0, False)
        for g in range(NG):
            if g + 1 < NG:
                front(sq, l, 0, g + 1, False)
            proj(g)

    def load_post(buf, gsrc, l):
        dma("sp", buf.ap, gsrc[l:l + 1, :].partition_broadcast(128)[:, 0, :], [b_IN], [buf])

    def post_tile(sq, l, tile, banks, gp, mid_in):
        b0 = banks[0]
        xt = XT[nxt("xt") % 3]
        st = STAT[nxt("st") % 16]
        tmp = TMP[nxt("tmp") % 2]
        src, sb = xsrc(sq, l, tile, mid_in)
        dma("sp", xt.ap, src, [sb], [xt])
        for hf in range(2):
            jk = JK[nxt("jk") % 3]
            S.add("act", lambda e, hf=hf, st=st, jk=jk: e.activation(out=jk.ap[:, 0:512], in_=ps[:, b0 + hf, :],
                                                                     func=AF.Square,
                                                                     accum_out=st.ap[:, 3 + hf:4 + hf]),
                  W=[(st, 3 + hf), jk], X=[PB[b0 + hf]])
        S.add("dve", lambda e, st=st: e.tensor_tensor(out=st.ap[:, 0:1], in0=st.ap[:, 3:4], in1=st.ap[:, 4:5],
                                                      op=ALU.add), R=[(st, 3), (st, 4)], W=[(st, 0)])
        rstd_ops(st, 1, D)
        S.add("dve", lambda e, st=st, tmp=tmp: e.scalar_tensor_tensor(
            out=tmp.ap, in0=ps[:, b0:b0 + 2, :].rearrange("p a b -> p (a b)"), scalar=st.ap[:, 2:3], in1=gp.ap,
            op0=ALU.mult, op1=ALU.mult), R=[(st, 2), gp], W=[tmp], X=[PB[b0], PB[b0 + 1]])
        S.add("dve", lambda e, xt=xt, tmp=tmp: e.tensor_tensor(out=xt.ap, in0=tmp.ap, in1=xt.ap, op=ALU.add),
              R=[tmp], W=[xt])
        dma("pool", sq["y"][tile * 128:(tile + 1) * 128, :], xt.ap, [xt], [sq["yb"][tile]])

    def P3(sq, l, OSB):
        NT = sq["S"] // 128
        OT = [A.reg(f"ot{i}", [KC, 128], BF16) for i in range(2)]
        wob = [(2, 3), (4, 5), (6, 7)]

        def trp(tile):
            tb = nxt("tr") % 2
            ot = OT[nxt("ot") % 2]

            def tr(e):
                for kc in range(KC):
                    i = e.transpose(trv(tb)[:, kc, :], OSB.ap[:, tile, kc * 128:(kc + 1) * 128], IDB)
                return i
            S.add("pe", tr, R=[(OSB, tile), CSTB], W=[PB[tb]])
            S.add("act", lambda e: e.activation(out=ot.ap, in_=trv(tb), func=AF.Copy), W=[ot], X=[PB[tb]])
            return ot

        nxt_ot = trp(0)
        for tile in range(NT):
            ot = nxt_ot
            if tile + 1 < NT:
                nxt_ot = trp(tile + 1)
            bk = wob[nxt("wob") % 3]

            def mm(e, ot=ot, bk=bk):
                for hf in range(2):
                    for kc in range(KC):
                        i = e.matmul(ps[:, bk[hf], :], lhsT=ot.ap[:, kc, :], rhs=WO.ap[:, kc, hf * 512:(hf + 1) * 512],
                                     start=(kc == 0), stop=(kc == KC - 1))
                return i
            S.add("pe", mm, R=[ot, WO], W=[PB[bk[0]], PB[bk[1]]])
            post_tile(sq, l, tile, bk, GPM, False)

    def P2A(sq, l):
        kind, la = layer_kind(l)
        Sq = sq["S"]
        NT = Sq // 128
        NG = Sq // 512
        A.reset_region()
        S.phase_switch()
        OSB = A.reg("osb", [NT, D], BF16)
        reg_mark = A.reg_top
        QA = [A.reg(f"qa{i}", [8, 512], BF16) for i in range(2)]
        KA = [A.reg(f"ka{i}", [4, 768], BF16) for i in range(2)]
        VAs = [A.reg(f"vas{i}", [6, 4, 65], BF16) for i in range(2)]
        PTA = [A.reg(f"pta{i}", [3, 256], BF16) for i in range(3)]
        DEN = [A.reg(f"den{i}", [8], F32) for i in range(4)]
        dma("sp", WO.ap, WOA[la].rearrange("(kc p) n -> p kc n", p=128), [b_WOA[la]], [WO])
        load_post(GPM, g_mix_post, l)
        scsets = [(0, 1, 2), (3, 4, 5)]
        def group(g):
            qa, ka, va = QA[g % 2], KA[g % 2], VAs[g % 2]
            tlo, thi = max(0, 4 * g - 1), min(NT - 1, 4 * g + 4)
            rlo = tlo - (4 * g - 1)
            nt = thi - tlo + 1
            dma("sp", qa.ap, QT[:, :, g * 512:(g + 1) * 512].rearrange("c p s -> p c s"),
                [(b_QT[c], g) for c in range(8)], [qa])
            dma("sp", ka.ap[:, :, rlo * 128:(rlo + nt) * 128],
                KT[0:4, :, tlo * 128:(thi + 1) * 128].rearrange("c p s -> p c s"),
                [b_KT[c] for c in range(4)], [ka])
            dma("sp", va.ap[:, rlo:rlo + nt, :, :], VA[:, tlo:thi + 1, :, :], [b_VA], [va])
            def qblock(qb):
                i = 4 * g + qb
                vb = [b for b in range(3) if 0 <= i - 1 + b < NT]
                b0, b1 = vb[0], vb[-1]

                def qk(j, w):
                    sc = scsets[w]
                    pt = PTA[nxt("pta") % 3]

                    def mm(e):
                        for b in vb:
                            r = qb + b
                            ins = e.matmul(ps[:, sc[b], 0:256],
                                           lhsT=ka.ap[64 * w:64 * w + 64, j, r * 128:(r + 1) * 128],
                                           rhs=qa.ap[64 * w:64 * w + 64, 2 * j:2 * j + 2, qb * 128:(qb + 1) * 128],
                                           start=True, stop=True)
                        return ins
                    S.add("pe", mm, R=[qa, ka], W=[PB[sc[b]] for b in vb])
                    S.add("act", lambda e: e.activation(out=pt.ap[:, b0:b1 + 1, :],
                                                        in_=ps[:, sc[b0]:sc[b1] + 1, 0:256],
                                                        func=AF.Exp, scale=0.125),
                          W=[pt], X=[PB[sc[b]] for b in vb])
                    for b, mk in ((0, MKP), (2, MKN)):
                        if b in vb:
                            v3 = pt.ap[:, b, :].rearrange("p (s q) -> p s q", s=2)
                            S.add("dve", lambda e, v3=v3, mk=mk: e.tensor_tensor(
                                out=v3, in0=v3, in1=mk.unsqueeze(1).to_broadcast([128, 2, 128]), op=ALU.mult),
                                R=[CSTB], W=[pt])
                    return pt

                def pv(j, w, pt):
                    ob = 6 + nxt("oa") % 2

                    def mm(e):
                        first = True
                        for hh in range(2):
                            for b in vb:
                                r = qb + b
                                ins = e.matmul(ps[:, ob, hh * 128:hh * 128 + 65],
                                               lhsT=pt.ap[:, b, hh * 128:(hh + 1) * 128],
                                               rhs=va.ap[:, r, j, :], start=first, stop=(b == vb[-1]),
                                               skip_group_check=True)
                                first = False
                        return ins
                    S.add("pe", mm, R=[pt, va], W=[PB[ob]])
                    den = DEN[nxt("den") % 4]
                    oav = ps[:, ob, 0:256].rearrange("p (hh c) -> p hh c", hh=2)
                    S.add("dve", lambda e: e.tensor_tensor(
                        out=den.ap[:, 0:2], in0=oav[:, :, 64],
                        in1=ESK.ap[:, la, j, :].rearrange("p (hh w) -> p w hh", hh=2)[:, w, :], op=ALU.add),
                        R=[ESK], W=[(den, 0)], X=[PB[ob]])
                    S.add("dve", lambda e: e.reciprocal(out=den.ap[:, 4:6], in_=den.ap[:, 0:2]),
                          R=[(den, 0)], W=[(den, 1)])
                    h0 = 4 * j + w
                    oo = OSB.ap[:, i, :].rearrange("p (h e) -> p h e", e=64)[:, h0:h0 + 3:2, :]
                    S.add("dve", lambda e: e.tensor_tensor(
                        out=oo, in0=oav[:, :, 0:64],
                        in1=den.ap[:, 4:6].unsqueeze(2).to_broadcast([128, 2, 64]), op=ALU.mult),
                        R=[(den, 1)], W=[(OSB, i)], X=[PB[ob]])

                prev = None
                for j in range(4):
                    for w in range(2):
                        pt = qk(j, w)
                        if prev is not None:
                            pv(*prev)
                        prev = (j, w, pt)
                pv(*prev)

            for qb in range(4):
                qblock(qb)

        for g in range(NG):
            group(g)
        S.phase_switch()
        A.reg_top = reg_mark
        if DBG >= 4:
            P3(sq, l, OSB)

    def P2B(sq, l):
        kind, jb = layer_kind(l)
        Sq = sq["S"]
        NT = Sq // 128
        NG = Sq // 512
        NP = NT // 2
        A.reset_region()
        S.phase_switch()
        OSB = A.reg("osb", [NT, D], BF16)
        reg_mark = A.reg_top
        KH = [A.reg(f"kh{i}", [Sq], BF16) for i in range(2)]
        VH = [A.reg(f"vh{i}", [NT, 129], BF16) for i in range(2)]
        QH = [A.reg(f"qh{i}", [512], BF16) for i in range(3)]
        PT = [A.reg(f"pt{i}", [2, 512], BF16) for i in range(3)]
        O1S = A.reg("o1s", [4, 128], F32)
        OS = A.reg("os", [4, 128], F32)
        RC = [A.reg(f"rc{i}", [8], F32) for i in range(4)]
        dma("sp", WO.ap, WOB[jb].rearrange("(kc p) n -> p kc n", p=128), [b_WOB[jb]], [WO])
        load_post(GPM, g_mix_post, l)
        scp = [(0, 1), (2, 3)]
        oacc = [(4, 5), (6, 7)]

        def ov(c):
            return ps[:, oacc[c][0]:oacc[c][1] + 1, 0:258].rearrange("p b (q c) -> p b q c", c=129)

        def head(h):
            kh, vh = KH[h % 2], VH[h % 2]
            dma("sp", kh.ap, KT[h][:, 0:Sq], [b_KT[h]], [kh])
            dma("sp", vh.ap, VB[h][:, 0:NT, :], [b_VB], [vh])

            def qgroup(qg):
                qh = QH[nxt("qh") % 3]
                dma("sp", qh.ap, QT[h][:, qg * 512:(qg + 1) * 512], [(b_QT[h], qg)], [qh])

                def cmap(c):
                    ob = oacc[c]

                    def qk(p, c=c):
                        sc = scp[nxt("scb") % 2]
                        pt = PT[nxt("ptb") % 3]

                        def mm(e):
                            for kk in range(2):
                                kt = 2 * p + kk
                                ins = e.matmul(ps[:, sc[kk], :], lhsT=kh.ap[64 * c:64 * c + 64, kt * 128:(kt + 1) * 128],
                                               rhs=qh.ap[64 * c:64 * c + 64, :], start=True, stop=True)
                            return ins
                        S.add("pe", mm, R=[kh, qh], W=[PB[sc[0]], PB[sc[1]]])
                        S.add("act", lambda e: e.activation(out=pt.ap, in_=ps[:, sc[0]:sc[1] + 1, :], func=AF.Exp,
                                                            scale=0.125), W=[pt], X=[PB[sc[0]], PB[sc[1]]])
                        return pt

                    def pv(p, pt, ob=ob):
                        def mm(e):
                            for kk in range(2):
                                kt = 2 * p + kk
                                for qt in range(4):
                                    ins = e.matmul(ps[:, ob[qt // 2], (qt % 2) * 129:(qt % 2) * 129 + 129],
                                                   lhsT=pt.ap[:, kk, qt * 128:(qt + 1) * 128], rhs=vh.ap[:, kt, :],
                                                   start=(p == 0 and kk == 0 and qt % 2 == 0),
                                                   stop=(p == NP - 1 and kk == 1), skip_group_check=True)
                            return ins
                        S.add("pe", mm, R=[pt, vh], W=[PB[ob[0]], PB[ob[1]]])

                    prev = None
                    for p in range(NP):
                        pt = qk(p)
                        if prev is not None:
                            pv(*prev)
                        prev = (p, pt)
                    pv(*prev)
                    rc = RC[nxt("rc") % 4]
                    pbs = [PB[ob[0]], PB[ob[1]]]
                    rcv = rc.ap[:, 0:4].rearrange("p (b q) -> p b q", b=2)
                    if c == 0:
                        S.add("dve", lambda e, rcv=rcv: e.reciprocal(out=rcv, in_=ov(0)[:, :, :, 128]),
                              W=[(rc, 0)], X=pbs)
                        S.add("dve", lambda e, rcv=rcv: e.tensor_tensor(
                            out=O1S.ap.rearrange("p (b q) c -> p b q c", b=2), in0=ov(0)[:, :, :, 0:128],
                            in1=rcv.unsqueeze(3).to_broadcast([128, 2, 2, 128]), op=ALU.mult),
                            R=[(rc, 0)], W=[O1S], X=pbs)
                    else:
                        S.add("dve", lambda e, rcv=rcv: e.reciprocal(out=rcv, in_=ov(1)[:, :, :, 128]),
                              W=[(rc, 0)], X=pbs)
                        S.add("dve", lambda e, rc=rc: e.tensor_scalar(
                            out=rc.ap[:, 4:8], in0=rc.ap[:, 0:4], scalar1=NEGLAM.ap[:, jb:jb + 1], scalar2=None,
                            op0=ALU.mult), R=[(rc, 0), NEGLAM], W=[(rc, 1)])
                        S.add("dve", lambda e, rc=rc: e.tensor_tensor(
                            out=OS.ap.rearrange("p (b q) c -> p b q c", b=2), in0=ov(1)[:, :, :, 0:128],
                            in1=rc.ap[:, 4:8].rearrange("p (b q) -> p b q", b=2).unsqueeze(3).to_broadcast(
                                [128, 2, 2, 128]), op=ALU.mult), R=[(rc, 1)], W=[OS], X=pbs)
                        S.add("dve", lambda e: e.tensor_tensor(out=OS.ap, in0=OS.ap, in1=O1S.ap, op=ALU.add),
                              R=[O1S], W=[OS])
                        st = STAT[nxt("st") % 16]
                        st2 = STAT[nxt("st") % 16]
                        for qt in range(4):
                            jk = JK[nxt("jk") % 3]
                            S.add("act", lambda e, qt=qt, st=st, jk=jk: e.activation(
                                out=jk.ap[:, 0:128], in_=OS.ap[:, qt, :], func=AF.Square,
                                accum_out=st.ap[:, qt:qt + 1]), R=[OS], W=[(st, qt), jk])
                        S.add("act", lambda e, st=st: e.activation(out=st.ap[:, 4:8], in_=st.ap[:, 0:4], func=AF.Ln,
                                                                   scale=1.0 / 128, bias=EPSB.ap[:, 0:1]),
                              R=[(st, q_) for q_ in range(4)] + [EPSB], W=[(st, 4)])
                        S.add("act", lambda e, st=st, st2=st2: e.activation(out=st2.ap[:, 0:4], in_=st.ap[:, 4:8],
                                                                            func=AF.Exp, scale=-0.5),
                              R=[(st, 4)], W=[st2])
                        for qt in range(4):
                            tile = 4 * qg + qt
                            S.add("dve", lambda e, qt=qt, tile=tile, st2=st2: e.scalar_tensor_tensor(
                                out=OSB.ap[:, tile, h * 128:(h + 1) * 128], in0=OS.ap[:, qt, :],
                                scalar=st2.ap[:, qt:qt + 1], in1=SUBG.ap[:, jb, :], op0=ALU.mult, op1=ALU.mult),
                                R=[OS, st2, SUBG], W=[(OSB, tile)])

                for c in range(2):
                    cmap(c)

            for qg in range(NG):
                qgroup(qg)

        for h in range(8):
            head(h)
        S.phase_switch()
        A.reg_top = reg_mark
        P3(sq, l, OSB)

    def P4(sq, l):
        Sq = sq["S"]
        NG = Sq // 512
        A.reset_region()
        S.phase_switch()
        WD = A.reg("wd", [NFF, D], BF16)
        AT = A.reg("at", [NFF, 512], BF16)
        SG = [A.reg(f"sg{i}", [512], F32) for i in range(2)]
        dma("sp", WD.ap, WDN[l].rearrange("(kc p) n -> p kc n", p=128), [b_WDN[l]], [WD])
        load_post(GPF, g_ffn_post, l)
        gub = [(2, 3), (4, 5)]
        dnb = [(6, 7), (4, 5)]

        def gate_up(g):
            ht = HT[g % 2]
            for j in range(NFF):
                ws = WS[nxt("ws") % 4]
                dma("sp", ws.ap, WGU[l][j], [b_WGU[l]], [ws])
                bk = gub[nxt("gub") % 2]
                sg = SG[nxt("sg") % 2]

                def mm(e, ws=ws, bk=bk):
                    for hf in range(2):
                        for kc in range(KC):
                            i = e.matmul(ps[:, bk[hf], :], lhsT=ws.ap[:, kc, hf * 128:(hf + 1) * 128],
                                         rhs=ht.ap[:, kc, :], start=(kc == 0), stop=(kc == KC - 1))
                    return i
                S.add("pe", mm, R=[ws, ht], W=[PB[bk[0]], PB[bk[1]]])
                S.add("act", lambda e, sg=sg, bk=bk: e.activation(out=sg.ap, in_=ps[:, bk[0], :], func=AF.Silu),
                      W=[sg], X=[PB[bk[0]]])
                S.add("dve", lambda e, sg=sg, bk=bk, j=j: e.tensor_tensor(out=AT.ap[:, j, :], in0=sg.ap,
                                                                          in1=ps[:, bk[1], :], op=ALU.mult),
                      R=[sg], W=[(AT, j)], X=[PB[bk[1]]])

        def down(g):
            for t in range(4):
                tile = 4 * g + t
                bk = dnb[nxt("dnb") % 2]

                def mm(e, t=t, bk=bk):
                    for hf in range(2):
                        for kc in range(NFF):
                            i = e.matmul(ps[:, bk[hf], :], lhsT=AT.ap[:, kc, t * 128:(t + 1) * 128],
                                         rhs=WD.ap[:, kc, hf * 512:(hf + 1) * 512],
                                         start=(kc == 0), stop=(kc == NFF - 1))
                    return i
                S.add("pe", mm, R=[AT, WD], W=[PB[bk[0]], PB[bk[1]]])
                post_tile(sq, l, tile, bk, GPF, True)

        front(sq, l, 1, 0, True)
        for g in range(NG):
            gate_up(g)
            if g + 1 < NG:
                front(sq, l, 1, g + 1, True)
            down(g)

    stop = cfg.get("stop", 99)
    prep()
    nph = 0
    for sq in seqs:
        for l in range(DEPTH):
            for ph in (P1, P2A if l % 2 == 0 else P2B, P4):
                nph += 1
                if nph <= stop:
                    ph(sq, l)
    S.add("sp", None, R=[b for sq in seqs for b in sq["yb"]])
    S.finalize()

    sems = {}
    for e_ in ("pe", "act", "dve", "pool"):
        sems[e_] = nc.alloc_semaphore(name="s_" + e_)
    for q_ in ("sp", "pool"):
        for i in range(NDMA_SEM):
            sems[(q_, i)] = nc.alloc_semaphore(name=f"d_{q_}{i}")
    nw = {}
    with nc.Block() as block:
        @block.sync
        def _(e):
            nw["sp"] = S.emit("sp", e, sems)

        @block.tensor
        def _(e):
            nw["pe"] = S.emit("pe", e, sems)

        @block.scalar
        def _(e):
            nw["act"] = S.emit("act", e, sems)

        @block.vector
        def _(e):
            nw["dve"] = S.emit("dve", e, sems)

        @block.gpsimd
        def _(e):
            nw["pool"] = S.emit("pool", e, sems)
    if cfg.get("verbose"):
        print("ops", {e: len(S.q[e]) for e in ENGS}, "waits", nw, "arena top", A.shared_top)
    return nc


def make_consts(smax):
    inv = (1.0 / (10000.0 ** (np.arange(0, 64, 2, dtype=np.float32) / np.float32(64)))).astype(np.float32)
    ang = np.arange(smax, dtype=np.float32)[:, None] * inv[None, :]
    cos = np.cos(ang).astype(np.float32)
    sin = np.sin(ang).astype(np.float32)
    idx = (np.arange(128) % 64) % 32
    cos_t = np.ascontiguousarray(cos[:, idx].T)
    sin_t = np.ascontiguousarray(sin[:, idx].T)
    cst = np.zeros((128, 4, 128), np.float32)
    cst[:, 0, :] = np.eye(128, dtype=np.float32)
    for m in range(128):
        if (m % 64) < 32:
            cst[m + 32, 1, m] = -1.0
        else:
            cst[m - 32, 1, m] = 1.0
    k = np.arange(128)[:, None]
    q = np.arange(128)[None, :]
    cst[:, 2, :] = (k >= q).astype(np.float32)
    cst[:, 3, :] = (k <= q).astype(np.float32)
    return cos_t, sin_t, cst


W_NAMES = ["norm_mix_pre", "norm_mix_post", "norm_ffn_pre", "norm_ffn_post", "a_w_in", "a_w_o", "a_sinks",
           "b_w_in", "b_w_o", "b_lambda_q1", "b_lambda_k1", "b_lambda_q2", "b_lambda_k2", "b_subln",
           "ffn_w_gate_up", "ffn_w_down"]

_CACHE = {}


def run(inputs, n_cores=8, verbose=False, trace=False, stop=99):
    xp = np.asarray(inputs["x_prompt"], np.float32)
    xs = np.asarray(inputs["x_sample"], np.float32)
    depth = inputs["norm_mix_pre"].shape[0]
    npc, nsc = xp.shape[0] // n_cores, xs.shape[0] // n_cores
    cfg = dict(depth=depth, seqs=[(xp.shape[1], npc), (xs.shape[1], nsc)], verbose=verbose, stop=stop)
    key = (depth, xp.shape, xs.shape, stop)
    if key not in _CACHE:
        _CACHE[key] = build(cfg)
    nc = _CACHE[key]
    smax = max(xp.shape[1], xs.shape[1])
    cos_t, sin_t, cst = make_consts(smax)
    shared = {k: np.ascontiguousarray(np.asarray(inputs[k], np.float32)) for k in W_NAMES
              if not (depth < 2 and k.startswith("b_"))}
    shared.update(cos_t=cos_t, sin_t=sin_t, cst=cst)
    in_maps = []
    for c in range(n_cores):
        m = dict(shared)
        m["x0"] = np.ascontiguousarray(xp[c * npc:(c + 1) * npc])
        m["x1"] = np.ascontiguousarray(xs[c * nsc:(c + 1) * nsc])
        in_maps.append(m)
    res = run_bass_kernel_spmd(nc, in_maps, core_ids=list(range(n_cores)), trace=trace)
    yp = np.concatenate([res.results[c]["y0"] for c in range(n_cores)], axis=0).astype(np.float32)
    ys = np.concatenate([res.results[c]["y1"] for c in range(n_cores)], axis=0).astype(np.float32)
    return (yp, ys), res


def kernel(**inputs):
    out, _ = run(inputs)
    return out
```

```python
import math
import numpy as np
import concourse.bass as bass
import concourse.mybir as mybir
from concourse.bass_utils import run_bass_kernel_spmd

F32 = mybir.dt.float32
BF16 = mybir.dt.bfloat16
AF = mybir.ActivationFunctionType
ALU = mybir.AluOpType
AX = mybir.AxisListType

D = 1024
KC = 8
DFF = 2816
NFF = 22
EPS = 1e-6
NDMA_SEM = 40
import os
DBG = int(os.environ.get('KDBG', '9'))
KV = int(os.environ.get('KV', '0'))
NWARM = int(os.environ.get('NWARM', '0'))
ENGS = ("pe", "act", "dve", "pool", "sp")


class Buf:
    __slots__ = ("name", "w", "r", "region")

    def __init__(self, name, region=False):
        self.name = name
        self.w = {}
        self.r = {}
        self.region = region


class Op:
    __slots__ = ("eng", "fn", "deps", "flag", "cnt", "dma", "dsem", "dval", "xb")


class Sched:
    def __init__(self):
        self.q = {e: [] for e in ENGS}
        self.ndma = {e: 0 for e in ENGS}
        self.dma_hist = {e: [] for e in ENGS}
        self.reg_eng = {}
        self.reg_dma = {}
        self.barrier = []
        self.barrier_done = set(ENGS)
        self.nops = 0

    @staticmethod
    def _norm(items):
        out = []
        for it in items:
            if isinstance(it, tuple):
                out.append(it)
            else:
                out.append((it, None))
        return out

    def phase_switch(self):
        self.barrier = list(self.reg_eng.values()) + list(self.reg_dma.values())
        self.barrier_done = set()

    def add(self, eng, fn, R=(), W=(), X=(), dma=False):
        op = Op()
        op.eng, op.fn, op.dma, op.flag, op.cnt = eng, fn, dma, False, 0
        op.dsem = op.dval = None
        deps = []
        R = self._norm(R)
        X = self._norm(X)
        op.xb = set(id(b) for b, _ in X)
        W = self._norm(W) + X
        touches_region = False
        for b, t in R:
            touches_region |= b.region
            if t is None:
                for w in b.w.values():
                    deps.append((w, True))
            else:
                w = b.w.get(t)
                if w is not None:
                    deps.append((w, True))
                w = b.w.get(None)
                if w is not None:
                    deps.append((w, True))
        for b, t in W:
            touches_region |= b.region
            isx = id(b) in op.xb
            if t is None:
                for w in b.w.values():
                    deps.append((w, 2 if (isx and id(b) in w.xb) else False))
                for rd in b.r.values():
                    for r in rd.values():
                        deps.append((r, False))
            else:
                for tt in (t, None):
                    w = b.w.get(tt)
                    if w is not None:
                        deps.append((w, 2 if (isx and id(b) in w.xb) else False))
                    rd = b.r.get(tt)
                    if rd:
                        for r in rd.values():
                            deps.append((r, False))
        if touches_region and eng not in self.barrier_done:
            self.barrier_done.add(eng)
            for d in self.barrier:
                deps.append((d, False))
        if dma:
            k = self.ndma[eng]
            op.dsem = (eng, k % NDMA_SEM)
        rkey = op.dsem if dma else eng
        for b, t in R:
            b.r.setdefault(t, {})[rkey] = op
        for b, t in W:
            if t is None:
                b.w = {None: op}
                b.r = {}
            else:
                b.w[t] = op
                b.r[t] = {}
        if dma:
            k = self.ndma[eng]
            self.ndma[eng] = k + 1
            op.dsem = (eng, k % NDMA_SEM)
            op.dval = 16 * (k // NDMA_SEM + 1)
            hist = self.dma_hist[eng]
            if k >= NDMA_SEM:
                deps.append((hist[k - NDMA_SEM], False))
            hist.append(op)
            if touches_region:
                self.reg_dma[op.dsem] = op
        elif touches_region:
            self.reg_eng[eng] = op
        op.deps = deps
        self.q[eng].append(op)
        self.nops += 1
        return op

    def finalize(self):
        for e in ENGS:
            for op in self.q[e]:
                for d, raw in op.deps:
                    if d.dma:
                        continue
                    if d.eng == e and (e == "pe" or raw == 2):
                        continue
                    d.flag = True
        for e in ENGS:
            c = 0
            for op in self.q[e]:
                if op.dma:
                    continue
                if op.flag:
                    c += 1
                    op.cnt = c

    def emit(self, eng, handle, sems):
        waited = {}
        nw = 0
        for op in self.q[eng]:
            for d, raw in op.deps:
                if d.dma:
                    key, val = d.dsem, d.dval
                else:
                    if d.eng == eng and (eng == "pe" or raw == 2):
                        continue
                    key, val = d.eng, d.cnt
                if waited.get(key, 0) >= val:
                    continue
                handle.wait_ge(sems[key], val)
                waited[key] = val
                nw += 1
            if op.fn is None:
                continue
            inst = op.fn(handle)
            if op.dma:
                inst.then_inc(sems[op.dsem], 16)
            elif op.flag:
                inst.then_inc(sems[eng], 1)
        return nw


class SB(Buf):
    __slots__ = ("ap", "nbytes")

    def __init__(self, name, ap, nbytes, region=False):
        Buf.__init__(self, name, region)
        self.ap = ap
        self.nbytes = nbytes


class Arena:
    def __init__(self, nc, nbytes):
        self.h = nc.alloc_sbuf_tensor("arena", [128, nbytes // 2], BF16)
        self.ap = self.h.ap()
        self.nbytes = nbytes
        self.shared_top = 0
        self.reg_top = 0

    def _view(self, off, shape, dt):
        n = 1
        for s in shape:
            n *= s
        esz = 4 if dt == F32 else 2
        nb = n * esz
        nb_al = (nb + 31) // 32 * 32
        assert off + nb_al <= self.nbytes, ("SBUF arena overflow", off, nb_al, self.nbytes)
        a = self.ap[:, off // 2: off // 2 + nb // 2]
        if dt == F32:
            a = a.bitcast(F32)
        if len(shape) == 2:
            a = a.rearrange("p (a b) -> p a b", a=shape[0])
        elif len(shape) == 3:
            a = a.rearrange("p (a b c) -> p a b c", a=shape[0], b=shape[1])
        elif len(shape) == 4:
            a = a.rearrange("p (a b c d) -> p a b c d", a=shape[0], b=shape[1], c=shape[2])
        return a, nb_al

    def shared(self, name, shape, dt):
        a, nb = self._view(self.shared_top, shape, dt)
        self.shared_top += nb
        self.reg_top = self.shared_top
        return SB(name, a, nb, False)

    def reset_region(self):
        self.reg_top = self.shared_top

    def reg(self, name, shape, dt):
        a, nb = self._view(self.reg_top, shape, dt)
        self.reg_top += nb
        return SB(name, a, nb, True)


def lambda_init_of(layer_idx):
    return 0.8 - 0.6 * math.exp(-0.3 * layer_idx)


def build(cfg):
    DEPTH = cfg["depth"]
    NA = (DEPTH + 1) // 2
    NB = DEPTH // 2
    seqdefs = cfg["seqs"]
    SMAX = max(s for s, _ in seqdefs)
    NTMAX = SMAX // 128

    nc = bass.Bass("TRN2", target_bir_lowering=False)
    S = Sched()

    def din(name, shape):
        return nc.dram_tensor(name, list(shape), F32, kind="ExternalInput").ap()

    xin = [din(f"x{i}", (n, s, D)) for i, (s, n) in enumerate(seqdefs)]
    yout = [nc.dram_tensor(f"y{i}", [n, s, D], F32, kind="ExternalOutput").ap()
            for i, (s, n) in enumerate(seqdefs)]
    g_mix_pre = din("norm_mix_pre", (DEPTH, D))
    g_mix_post = din("norm_mix_post", (DEPTH, D))
    g_ffn_pre = din("norm_ffn_pre", (DEPTH, D))
    g_ffn_post = din("norm_ffn_post", (DEPTH, D))
    a_w_in = din("a_w_in", (NA, D, 1536))
    a_w_o = din("a_w_o", (NA, D, D))
    a_sinks = din("a_sinks", (NA, 16))
    if NB:
        b_w_in = din("b_w_in", (NB, D, 3072))
        b_w_o = din("b_w_o", (NB, D, D))
        b_lq1 = din("b_lambda_q1", (NB, 64))
        b_lk1 = din("b_lambda_k1", (NB, 64))
        b_lq2 = din("b_lambda_q2", (NB, 64))
        b_lk2 = din("b_lambda_k2", (NB, 64))
        b_subln = din("b_subln", (NB, 128))
    w_gu = din("ffn_w_gate_up", (DEPTH, D, 2 * DFF))
    w_dn = din("ffn_w_down", (DEPTH, DFF, D))
    cosT = din("cos_t", (128, SMAX))
    sinT = din("sin_t", (128, SMAX))
    cst = din("cst", (128, 4, 128))

    def scr(name, shape):
        return nc.dram_tensor(name, list(shape), BF16, kind="Internal").ap()

    WINA = [scr(f"wina{l}", (7, 128, KC, 256)) for l in range(NA)]
    WOA = [scr(f"woa{l}", (D, D)) for l in range(NA)]
    WINB = [scr(f"winb{l}", (12, 128, KC, 256)) for l in range(NB)]
    WOB = [scr(f"wob{l}", (D, D)) for l in range(NB)]
    WGU = [scr(f"wgu{i}", (NFF, 128, KC, 256)) for i in range(DEPTH)]
    WDN = [scr(f"wdn{i}", (DFF, D)) for i in range(DEPTH)]
    QT = scr("qt", (8, 128, SMAX))
    KT = scr("kt", (8, 128, SMAX))
    VB = scr("vb", (8, 128, NTMAX, 129))
    VA = scr("va", (128, NTMAX, 4, 65))

    b_WINA = [Buf(f"wina{l}") for l in range(NA)]
    b_WOA = [Buf(f"woa{l}") for l in range(NA)]
    b_WINB = [Buf(f"winb{l}") for l in range(NB)]
    b_WOB = [Buf(f"wob{l}") for l in range(NB)]
    b_WGU = [Buf(f"wgu{i}") for i in range(DEPTH)]
    b_WDN = [Buf(f"wdn{i}") for i in range(DEPTH)]
    b_QT = [Buf(f"qt{c}") for c in range(8)]
    b_KT = [Buf(f"kt{c}") for c in range(8)]
    b_VB = Buf("vb")
    b_VA = Buf("va")
    b_IN = Buf("inputs")

    seqs = []
    for i, (s, n) in enumerate(seqdefs):
        for k in range(n):
            seqs.append(dict(S=s, xin=xin[i][k], y=yout[i][k],
                             yb=[Buf(f"y{i}_{k}_{t}") for t in range(s // 128)]))

    A = Arena(nc, cfg.get("arena", 204 * 1024))
    ps = nc.alloc_psum_tensor("ps", [128, 8, 512], F32).ap()
    PB = [Buf(f"psum{b}") for b in range(8)]

    def trv(b):
        return ps[:, b, :].bitcast(BF16).rearrange("p (k t) -> p k t", k=8)

    CSTB = A.shared("cstb", [4, 128], BF16)
    IDB = CSTB.ap[:, 0, :]
    RMB = CSTB.ap[:, 1, :]
    MKP = CSTB.ap[:, 2, :]
    MKN = CSTB.ap[:, 3, :]
    GPRE = A.shared("gpre", [2, DEPTH, 8], F32)
    ESK = A.shared("esk", [NA, 4, 4], F32)
    NEGLAM = A.shared("neglam", [max(NB, 1)], F32)
    SUBG = A.shared("subg", [max(NB, 1), 128], F32)
    EPSB = A.shared("epsb", [8], F32)
    GPM = A.shared("gpm", [D], F32)
    GPF = A.shared("gpf", [D], F32)
    XT = [A.shared(f"xt{i}", [D], F32) for i in range(3)]
    XN = [A.shared(f"xn{i}", [D], BF16) for i in range(2)]
    JK = [A.shared(f"junk{i}", [D], BF16) for i in range(3)]
    HT = [A.shared(f"ht{i}", [KC, 512], BF16) for i in range(2)]
    STAT = [A.shared(f"stat{i}", [8], F32) for i in range(16)]
    WS = [A.shared(f"ws{i}", [KC, 256], BF16) for i in range(4)]
    WO = A.shared("wo", [KC, D], BF16)
    TMP = [A.shared(f"tmp{i}", [D], F32) for i in range(2)]

    ctr = {}

    def nxt(k):
        v = ctr.get(k, 0)
        ctr[k] = v + 1
        return v

    def dma(q, out, in_, R, W, **kw):
        return S.add(q, lambda e, o=out, i=in_, kw=kw: e.dma_start(out=o, in_=i, **kw), R=R, W=W, dma=True)

    def layer_kind(l):
        return ("A", l // 2) if l % 2 == 0 else ("B", l // 2)

    def prep():
        A.reset_region()
        S.phase_switch()
        F = [A.reg(f"pf{i}", [3072], F32) for i in range(2)]
        Bq = [A.reg(f"pb{i}", [3072], BF16) for i in range(2)]
        SM = A.reg("psm", [1024], F32)
        dma("sp", F[0].ap[:, 0:512], cst.rearrange("p a b -> p (a b)"), [b_IN], [F[0]])
        S.add("dve", lambda e: e.tensor_copy(out=CSTB.ap.rearrange("p a b -> p (a b)"), in_=F[0].ap[:, 0:512]),
              R=[F[0]], W=[CSTB])
        S.add("dve", lambda e: e.memset(EPSB.ap, EPS), W=[EPSB])
        dma("sp", GPRE.ap[:, 0, :, :], g_mix_pre.rearrange("l (kc p) -> p l kc", p=128), [b_IN], [(GPRE, 0)],
            allow_slow_non_contiguous=True)
        dma("sp", GPRE.ap[:, 1, :, :], g_ffn_pre.rearrange("l (kc p) -> p l kc", p=128), [b_IN], [(GPRE, 1)],
            allow_slow_non_contiguous=True)
        for la in range(NA):
            src = a_sinks[la:la + 1, :].partition_broadcast(128)[:, 0, :]
            dma("sp", SM.ap[:, la * 16:(la + 1) * 16], src, [b_IN], [(SM, la)])
        S.add("act", lambda e: e.activation(out=ESK.ap.rearrange("p a b c -> p (a b c)"), in_=SM.ap[:, 0:NA * 16],
                                            func=AF.Exp), R=[SM], W=[ESK])
        if NB:
            o = 64
            for nm, src in (("q1", b_lq1), ("k1", b_lk1), ("q2", b_lq2), ("k2", b_lk2)):
                s2 = src.rearrange("(o a) b -> o (a b)", o=1).partition_broadcast(128)[:, 0, :]
                dma("sp", SM.ap[:, o:o + NB * 64], s2, [b_IN], [(SM, nm)])
                o += NB * 64
            sg = b_subln.rearrange("(o a) b -> o (a b)", o=1).partition_broadcast(128)[:, 0, :]
            dma("sp", SM.ap[:, o:o + NB * 128], sg, [b_IN], [(SM, "sg")])
            osg = o
            q1 = SM.ap[:, 64:64 + NB * 64]
            k1 = SM.ap[:, 64 + NB * 64:64 + 2 * NB * 64]
            q2 = SM.ap[:, 64 + 2 * NB * 64:64 + 3 * NB * 64]
            k2 = SM.ap[:, 64 + 3 * NB * 64:64 + 4 * NB * 64]
            pr = A.reg("ppr", [2, NB, 64], F32)
            rd = A.reg("prd", [2, NB], F32)
            ex = A.reg("pex", [2, NB], F32)
            S.add("dve", lambda e: e.tensor_tensor(out=pr.ap[:, 0].rearrange("p a b -> p (a b)"), in0=q1, in1=k1,
                                                   op=ALU.mult), R=[SM], W=[(pr, 0)])
            S.add("dve", lambda e: e.tensor_tensor(out=pr.ap[:, 1].rearrange("p a b -> p (a b)"), in0=q2, in1=k2,
                                                   op=ALU.mult), R=[SM], W=[(pr, 1)])
            S.add("dve", lambda e: e.tensor_reduce(out=rd.ap.rearrange("p a b -> p (a b)"),
                                                   in_=pr.ap.rearrange("p a b c -> p (a b) c"),
                                                   axis=AX.X, op=ALU.add), R=[pr], W=[rd])
            S.add("act", lambda e: e.activation(out=ex.ap.rearrange("p a b -> p (a b)"),
                                                in_=rd.ap.rearrange("p a b -> p (a b)"), func=AF.Exp),
                  R=[rd], W=[ex])
            S.add("dve", lambda e: e.tensor_tensor(out=rd.ap[:, 0, :], in0=ex.ap[:, 1, :], in1=ex.ap[:, 0, :],
                                                   op=ALU.subtract), R=[ex], W=[rd])
            for j in range(NB):
                li = lambda_init_of(2 * j + 1)
                S.add("dve", lambda e, j=j, li=li: e.tensor_scalar_add(out=NEGLAM.ap[:, j:j + 1],
                                                                       in0=rd.ap[:, 0, j:j + 1], scalar1=-li),
                      R=[rd], W=[(NEGLAM, j)])
                S.add("dve", lambda e, j=j, li=li: e.tensor_scalar_mul(
                    out=SUBG.ap[:, j, :], in0=SM.ap[:, osg + j * 128: osg + (j + 1) * 128], scalar1=(1.0 - li)),
                    R=[SM], W=[(SUBG, j)])

        it = [0]

        def conv(src_ap, ncols, stores, wbuf):
            i = it[0]
            it[0] += 1
            f, b = F[i % 2], Bq[i % 2]
            dma("sp", f.ap[:, 0:ncols], src_ap, [b_IN], [f])
            if i % 2 == 0:
                S.add("dve", lambda e: e.tensor_copy(out=b.ap[:, 0:ncols], in_=f.ap[:, 0:ncols]), R=[f], W=[b])
            else:
                S.add("act", lambda e: e.activation(out=b.ap[:, 0:ncols], in_=f.ap[:, 0:ncols], func=AF.Copy),
                      R=[f], W=[b])
            for dst, srcv in stores(b.ap):
                dma("pool", dst, srcv, [b], [(wbuf, nxt("wtag"))])

        for l in range(NA):
            for kc in range(KC):
                def st(bap, l=l, kc=kc):
                    out = []
                    out.append((WINA[l][0:4, :, kc, :].rearrange("s p c -> p s c"),
                                bap[:, 0:1024].rearrange("p (s c) -> p s c", s=4)))
                    out.append((WINA[l][6, :, kc, :], bap[:, 1280:1536]))
                    for s_ in range(2):
                        for d in range(2):
                            dst = WINA[l][4 + s_, :, kc, :].rearrange("p (k d c) -> p k d c", k=2, d=2)[:, :, d, :]
                            out.append((dst, bap[:, 1024 + s_ * 128:1152 + s_ * 128].rearrange(
                                "p (k c) -> p k c", k=2)))
                    return out
                conv(a_w_in[l, kc * 128:(kc + 1) * 128, :], 1536, st, b_WINA[l])
            for kc in range(KC):
                conv(a_w_o[l, kc * 128:(kc + 1) * 128, :], D,
                     lambda bap, l=l, kc=kc: [(WOA[l][kc * 128:(kc + 1) * 128, :], bap[:, 0:D])], b_WOA[l])
        for l in range(NB):
            for kc in range(KC):
                conv(b_w_in[l, kc * 128:(kc + 1) * 128, :], 3072,
                     lambda bap, l=l, kc=kc: [(WINB[l][:, :, kc, :].rearrange("s p c -> p s c"),
                                               bap[:, 0:3072].rearrange("p (s c) -> p s c", s=12))], b_WINB[l])
            for kc in range(KC):
                conv(b_w_o[l, kc * 128:(kc + 1) * 128, :], D,
                     lambda bap, l=l, kc=kc: [(WOB[l][kc * 128:(kc + 1) * 128, :], bap[:, 0:D])], b_WOB[l])
        for i in range(DEPTH):
            for kc in range(KC):
                for half in range(2):
                    conv(w_gu[i, kc * 128:(kc + 1) * 128, half * DFF:(half + 1) * DFF], DFF,
                         lambda bap, i=i, kc=kc, half=half: [
                             (WGU[i][:, :, kc, half * 128:(half + 1) * 128].rearrange("j p c -> p j c"),
                              bap[:, 0:DFF].rearrange("p (j c) -> p j c", j=NFF))], b_WGU[i])
            for rc in range(NFF):
                conv(w_dn[i, rc * 128:(rc + 1) * 128, :], D,
                     lambda bap, i=i, rc=rc: [(WDN[i][rc * 128:(rc + 1) * 128, :], bap[:, 0:D])], b_WDN[i])

    def xsrc(sq, l, tile, mid):
        if l == 0 and not mid:
            return sq["xin"][tile * 128:(tile + 1) * 128, :], b_IN
        return sq["y"][tile * 128:(tile + 1) * 128, :], sq["yb"][tile]

    def rstd_ops(st, n, dim):
        S.add("act", lambda e: e.activation(out=st.ap[:, n:2 * n], in_=st.ap[:, 0:n], func=AF.Ln,
                                            scale=1.0 / dim, bias=EPSB.ap[:, 0:1]),
              R=[(st, 0), EPSB], W=[(st, 1)])
        S.add("act", lambda e: e.activation(out=st.ap[:, 2 * n:3 * n], in_=st.ap[:, n:2 * n], func=AF.Exp,
                                            scale=-0.5), R=[(st, 1)], W=[(st, 2)])

    def front(sq, l, which, g, mid):
        ht = HT[g % 2]
        for t in range(4):
            tile = 4 * g + t
            xt = XT[nxt("xt") % 3]
            st = STAT[nxt("st") % 16]
            xn = XN[nxt("xn") % 2]
            tb = nxt("tr") % 2
            src, sb = xsrc(sq, l, tile, mid)
            dma("sp", xt.ap, src, [sb], [xt])
            jk = JK[nxt("jk") % 3]
            S.add("act", lambda e, xt=xt, st=st, jk=jk: e.activation(out=jk.ap, in_=xt.ap, func=AF.Square,
                                                                     accum_out=st.ap[:, 0:1]),
                  R=[xt], W=[(st, 0), jk])
            rstd_ops(st, 1, D)
            S.add("dve", lambda e, xt=xt, st=st, xn=xn: e.tensor_scalar(
                out=xn.ap, in0=xt.ap, scalar1=st.ap[:, 2:3], scalar2=None, op0=ALU.mult),
                R=[xt, (st, 2)], W=[xn])

            def tr(e, xn=xn, tb=tb):
                for kc in range(KC):
                    i = e.transpose(trv(tb)[:, kc, :], xn.ap[:, kc * 128:(kc + 1) * 128], IDB)
                return i
            S.add("pe", tr, R=[xn, CSTB], W=[PB[tb]])
            S.add("dve", lambda e, ht=ht, t=t, tb=tb: e.tensor_tensor(
                out=ht.ap[:, :, t * 128:(t + 1) * 128], in0=trv(tb),
                in1=GPRE.ap[:, which, l, :].unsqueeze(2).to_broadcast([128, KC, 128]), op=ALU.mult),
                R=[GPRE], W=[(ht, t)], X=[PB[tb]])

    def P1(sq, l):
        kind, j = layer_kind(l)
        Sq = sq["S"]
        NG = Sq // 512
        A.reset_region()
        S.phase_switch()
        TC = [A.reg(f"tc{i}", [512], F32) for i in range(2)]
        TS = [A.reg(f"ts{i}", [512], F32) for i in range(2)]
        QR = [A.reg(f"qr{i}", [512], BF16) for i in range(3)]
        T1 = [A.reg(f"t1{i}", [512], F32) for i in range(2)]
        T2 = [A.reg(f"t2{i}", [512], F32) for i in range(2)]
        QO = [A.reg(f"qo{i}", [512], BF16) for i in range(4)]
        if kind == "A":
            VST = A.reg("vst", [4, 4, 65], BF16)
            WIN, bWIN = WINA[j], b_WINA[j]
            slots = [("q", s) for s in range(4)] + [("k", s) for s in range(2)] + [("v", 0)]
        else:
            VST = A.reg("vst", [8, 4, 129], BF16)
            WIN, bWIN = WINB[j], b_WINB[j]
            slots = [("q", s) for s in range(4)] + [("k", s) for s in range(4)] + [("v", s) for s in range(4)]
        S.add("dve", lambda e: e.memset(VST.ap, 1.0), W=[VST])

        def proj(g):
            ht = HT[g % 2]
            tc, ts = TC[g % 2], TS[g % 2]
            dma("sp", tc.ap, cosT[:, g * 512:(g + 1) * 512], [b_IN], [tc])
            dma("sp", ts.ap, sinT[:, g * 512:(g + 1) * 512], [b_IN], [ts])
            pend = []

            def second(ci):
                (qr, pj, dstT, dbuf, c) = ci
                rb = 5 + nxt("rot") % 3
                S.add("pe", lambda e: e.matmul(ps[:, rb, :], lhsT=RMB, rhs=qr.ap, start=True, stop=True),
                      R=[qr, CSTB], W=[PB[rb]])
                t1 = T1[nxt("t1") % 2]
                t2 = T2[nxt("t2") % 2]
                qo = QO[nxt("qo") % 4]
                S.add("dve", lambda e: e.tensor_tensor(out=t1.ap, in0=ps[:, rb, :], in1=ts.ap, op=ALU.mult),
                      R=[ts], W=[t1], X=[PB[rb]])
                S.add("dve", lambda e: e.tensor_tensor(out=t2.ap, in0=qr.ap, in1=tc.ap, op=ALU.mult),
                      R=[qr, tc], W=[t2])
                S.add("dve", lambda e: e.tensor_tensor(out=qo.ap, in0=t1.ap, in1=t2.ap, op=ALU.add),
                      R=[t1, t2], W=[qo])
                dma("pool", dstT[c][:, g * 512:(g + 1) * 512], qo.ap, [qo], [(dbuf[c], g)])

            for si, (ty, s) in enumerate(slots):
                ws = WS[nxt("ws") % 4]
                dma("sp", ws.ap, WIN[si], [bWIN], [ws])
                if ty in ("q", "k"):
                    for half in range(2):
                        c = 2 * s + half
                        pj = 2 + nxt("pj") % 3
                        qr = QR[nxt("qr") % 3]

                        def mm(e, ws=ws, half=half, pj=pj):
                            for kc in range(KC):
                                i = e.matmul(ps[:, pj, :], lhsT=ws.ap[:, kc, half * 128:(half + 1) * 128],
                                             rhs=ht.ap[:, kc, :], start=(kc == 0), stop=(kc == KC - 1))
                            return i
                        S.add("pe", mm, R=[ws, ht], W=[PB[pj]])
                        S.add("act", lambda e, qr=qr, pj=pj: e.activation(out=qr.ap, in_=ps[:, pj, :], func=AF.Copy),
                              W=[qr], X=[PB[pj]])
                        if pend:
                            second(pend.pop(0))
                        pend.append((qr, pj, QT if ty == "q" else KT, b_QT if ty == "q" else b_KT, c))
                else:
                    while pend:
                        second(pend.pop(0))
                    for t in range(4):
                        pj = 2 + nxt("pj") % 3

                        def mmv(e, ws=ws, t=t, pj=pj):
                            for kc in range(KC):
                                i = e.matmul(ps[:, pj, 0:256], lhsT=ht.ap[:, kc, t * 128:(t + 1) * 128],
                                             rhs=ws.ap[:, kc, :], start=(kc == 0), stop=(kc == KC - 1))
                            return i
                        S.add("pe", mmv, R=[ws, ht], W=[PB[pj]])
                        if kind == "A":
                            o_ap = VST.ap[:, t, :, 0:64]
                            i_ap = ps[:, pj, 0:256].rearrange("p (k c) -> p k c", k=4)
                        else:
                            o_ap = VST.ap[:, 2 * s:2 * s + 2, t, 0:128]
                            i_ap = ps[:, pj, 0:256].rearrange("p (k c) -> p k c", k=2)
                        S.add("act", lambda e, o_ap=o_ap, i_ap=i_ap: e.activation(out=o_ap, in_=i_ap, func=AF.Copy),
                              W=[(VST, (t, s))], X=[PB[pj]])
            while pend:
                second(pend.pop(0))
            if kind == "A":
                dma("pool", VA[:, 4 * g:4 * g + 4, :, :], VST.ap, [VST], [(b_VA, g)])
            else:
                dma("pool", VB[:, :, 4 * g:4 * g + 4, :].rearrange("h p t c -> p h t c"), VST.ap, [VST], [(b_VB, g)])

        front(sq, l, 0, 0, False)
        for g in range(NG):
            if g + 1 < NG:
                front(sq, l, 0, g + 1, False)
            proj(g)

    def load_post(buf, gsrc, l):
        dma("sp", buf.ap, gsrc[l:l + 1, :].partition_broadcast(128)[:, 0, :], [b_IN], [buf])

    def post_tile(sq, l, tile, banks, gp, mid_in):
        b0 = banks[0]
        xt = XT[nxt("xt") % 3]
        st = STAT[nxt("st") % 16]
        tmp = TMP[nxt("tmp") % 2]
        src, sb = xsrc(sq, l, tile, mid_in)
        dma("sp", xt.ap, src, [sb], [xt])
        for hf in range(2):
            jk = JK[nxt("jk") % 3]
            S.add("act", lambda e, hf=hf, st=st, jk=jk: e.activation(out=jk.ap[:, 0:512], in_=ps[:, b0 + hf, :],
                                                                     func=AF.Square,
                                                                     accum_out=st.ap[:, 3 + hf:4 + hf]),
                  W=[(st, 3 + hf), jk], X=[PB[b0 + hf]])
        S.add("dve", lambda e, st=st: e.tensor_tensor(out=st.ap[:, 0:1], in0=st.ap[:, 3:4], in1=st.ap[:, 4:5],
                                                      op=ALU.add), R=[(st, 3), (st, 4)], W=[(st, 0)])
        rstd_ops(st, 1, D)
        S.add("dve", lambda e, st=st, tmp=tmp: e.scalar_tensor_tensor(
            out=tmp.ap, in0=ps[:, b0:b0 + 2, :].rearrange("p a b -> p (a b)"), scalar=st.ap[:, 2:3], in1=gp.ap,
            op0=ALU.mult, op1=ALU.mult), R=[(st, 2), gp], W=[tmp], X=[PB[b0], PB[b0 + 1]])
        S.add("dve", lambda e, xt=xt, tmp=tmp: e.tensor_tensor(out=xt.ap, in0=tmp.ap, in1=xt.ap, op=ALU.add),
              R=[tmp], W=[xt])
        dma("pool", sq["y"][tile * 128:(tile + 1) * 128, :], xt.ap, [xt], [sq["yb"][tile]])

    def P3(sq, l, OSB):
        NT = sq["S"] // 128
        OT = [A.reg(f"ot{i}", [KC, 128], BF16) for i in range(2)]
        wob = [(2, 3), (4, 5), (6, 7)]

        def trp(tile):
            tb = nxt("tr") % 2
            ot = OT[nxt("ot") % 2]

            def tr(e):
                for kc in range(KC):
                    i = e.transpose(trv(tb)[:, kc, :], OSB.ap[:, tile, kc * 128:(kc + 1) * 128], IDB)
                return i
            S.add("pe", tr, R=[(OSB, tile), CSTB], W=[PB[tb]])
            S.add("act", lambda e: e.activation(out=ot.ap, in_=trv(tb), func=AF.Copy), W=[ot], X=[PB[tb]])
            return ot

        nxt_ot = trp(0)
        for tile in range(NT):
            ot = nxt_ot
            if tile + 1 < NT:
                nxt_ot = trp(tile + 1)
            bk = wob[nxt("wob") % 3]

            def mm(e, ot=ot, bk=bk):
                for hf in range(2):
                    for kc in range(KC):
                        i = e.matmul(ps[:, bk[hf], :], lhsT=ot.ap[:, kc, :], rhs=WO.ap[:, kc, hf * 512:(hf + 1) * 512],
                                     start=(kc == 0), stop=(kc == KC - 1))
                return i
            S.add("pe", mm, R=[ot, WO], W=[PB[bk[0]], PB[bk[1]]])
            post_tile(sq, l, tile, bk, GPM, False)

    def P2A(sq, l):
        kind, la = layer_kind(l)
        Sq = sq["S"]
        NT = Sq // 128
        NG = Sq // 512
        A.reset_region()
        S.phase_switch()
        OSB = A.reg("osb", [NT, D], BF16)
        reg_mark = A.reg_top
        QA = [A.reg(f"qa{i}", [8, 512], BF16) for i in range(2)]
        KA = [A.reg(f"ka{i}", [4, 768], BF16) for i in range(2)]
        VAs = [A.reg(f"vas{i}", [6, 4, 65], BF16) for i in range(2)]
        PTA = [A.reg(f"pta{i}", [3, 256], BF16) for i in range(3)]
        DEN = [A.reg(f"den{i}", [8], F32) for i in range(4)]
        dma("sp", WO.ap, WOA[la].rearrange("(kc p) n -> p kc n", p=128), [b_WOA[la]], [WO])
        load_post(GPM, g_mix_post, l)
        scsets = [(0, 1, 2), (3, 4, 5)]
        def group(g):
            qa, ka, va = QA[g % 2], KA[g % 2], VAs[g % 2]
            tlo, thi = max(0, 4 * g - 1), min(NT - 1, 4 * g + 4)
            rlo = tlo - (4 * g - 1)
            nt = thi - tlo + 1
            dma("sp", qa.ap, QT[:, :, g * 512:(g + 1) * 512].rearrange("c p s -> p c s"),
                [(b_QT[c], g) for c in range(8)], [qa])
            dma("sp", ka.ap[:, :, rlo * 128:(rlo + nt) * 128],
                KT[0:4, :, tlo * 128:(thi + 1) * 128].rearrange("c p s -> p c s"),
                [b_KT[c] for c in range(4)], [ka])
            dma("sp", va.ap[:, rlo:rlo + nt, :, :], VA[:, tlo:thi + 1, :, :], [b_VA], [va])
            def qblock(qb):
                i = 4 * g + qb
                vb = [b for b in range(3) if 0 <= i - 1 + b < NT]
                b0, b1 = vb[0], vb[-1]

                def qk(j, w):
                    sc = scsets[w]
                    pt = PTA[nxt("pta") % 3]

                    def mm(e):
                        for b in vb:
                            r = qb + b
                            ins = e.matmul(ps[:, sc[b], 0:256],
                                           lhsT=ka.ap[64 * w:64 * w + 64, j, r * 128:(r + 1) * 128],
                                           rhs=qa.ap[64 * w:64 * w + 64, 2 * j:2 * j + 2, qb * 128:(qb + 1) * 128],
                                           start=True, stop=True)
                        return ins
                    S.add("pe", mm, R=[qa, ka], W=[PB[sc[b]] for b in vb])
                    S.add("act", lambda e: e.activation(out=pt.ap[:, b0:b1 + 1, :],
                                                        in_=ps[:, sc[b0]:sc[b1] + 1, 0:256],
                                                        func=AF.Exp, scale=0.125),
                          W=[pt], X=[PB[sc[b]] for b in vb])
                    for b, mk in ((0, MKP), (2, MKN)):
                        if b in vb:
                            v3 = pt.ap[:, b, :].rearrange("p (s q) -> p s q", s=2)
                            S.add("dve", lambda e, v3=v3, mk=mk: e.tensor_tensor(
                                out=v3, in0=v3, in1=mk.unsqueeze(1).to_broadcast([128, 2, 128]), op=ALU.mult),
                                R=[CSTB], W=[pt])
                    return pt

                def pv(j, w, pt):
                    ob = 6 + nxt("oa") % 2

                    def mm(e):
                        first = True
                        for hh in range(2):
                            for b in vb:
                                r = qb + b
                                ins = e.matmul(ps[:, ob, hh * 128:hh * 128 + 65],
                                               lhsT=pt.ap[:, b, hh * 128:(hh + 1) * 128],
                                               rhs=va.ap[:, r, j, :], start=first, stop=(b == vb[-1]),
                                               skip_group_check=True)
                                first = False
                        return ins
                    S.add("pe", mm, R=[pt, va], W=[PB[ob]])
                    den = DEN[nxt("den") % 4]
                    oav = ps[:, ob, 0:256].rearrange("p (hh c) -> p hh c", hh=2)
                    S.add("dve", lambda e: e.tensor_tensor(
                        out=den.ap[:, 0:2], in0=oav[:, :, 64],
                        in1=ESK.ap[:, la, j, :].rearrange("p (hh w) -> p w hh", hh=2)[:, w, :], op=ALU.add),
                        R=[ESK], W=[(den, 0)], X=[PB[ob]])
                    S.add("dve", lambda e: e.reciprocal(out=den.ap[:, 4:6], in_=den.ap[:, 0:2]),
                          R=[(den, 0)], W=[(den, 1)])
                    h0 = 4 * j + w
                    oo = OSB.ap[:, i, :].rearrange("p (h e) -> p h e", e=64)[:, h0:h0 + 3:2, :]
                    S.add("dve", lambda e: e.tensor_tensor(
                        out=oo, in0=oav[:, :, 0:64],
                        in1=den.ap[:, 4:6].unsqueeze(2).to_broadcast([128, 2, 64]), op=ALU.mult),
                        R=[(den, 1)], W=[(OSB, i)], X=[PB[ob]])

                prev = None
                for j in range(4):
                    for w in range(2):
                        pt = qk(j, w)
                        if prev is not None:
                            pv(*prev)
                        prev = (j, w, pt)
                pv(*prev)

            for qb in range(4):
                qblock(qb)

        for g in range(NG):
            group(g)
        S.phase_switch()
        A.reg_top = reg_mark
        if DBG >= 4:
            P3(sq, l, OSB)

    def P2B(sq, l):
        kind, jb = layer_kind(l)
        Sq = sq["S"]
        NT = Sq // 128
        NG = Sq // 512
        NP = NT // 2
        A.reset_region()
        S.phase_switch()
        OSB = A.reg("osb", [NT, D], BF16)
        reg_mark = A.reg_top
        KH = [A.reg(f"kh{i}", [Sq], BF16) for i in range(2)]
        VH = [A.reg(f"vh{i}", [NT, 129], BF16) for i in range(2)]
        QP = [[A.reg(f"qp{c}_{i}", [512], BF16) for i in range(2)] for c in range(2)]
        PT = [A.reg(f"pt{i}", [2, 512], BF16) for i in range(4)]
        O1S = A.reg("o1s", [4, 128], F32)
        OS = A.reg("os", [4, 128], F32)
        RC = [A.reg(f"rc{i}", [8], F32) for i in range(4)]
        dma("sp", WO.ap, WOB[jb].rearrange("(kc p) n -> p kc n", p=128), [b_WOB[jb]], [WO])
        load_post(GPM, g_mix_post, l)
        scp = [(0, 1), (2, 3), (4, 5)]
        oacc = [(6, 7), (6, 7)]
        pipe = []
        LOOK = 2

        def push(qk_fn, pv_fn):
            pt = qk_fn()
            pipe.append((pv_fn, pt))
            if len(pipe) > LOOK:
                f, a = pipe.pop(0)
                f(a)

        def flush():
            while pipe:
                f, a = pipe.pop(0)
                f(a)

        for c_ in range(2):
            for qp_ in QP[c_]:
                S.add("dve", lambda e, qp_=qp_: e.memset(qp_.ap, 0.0), W=[qp_])

        def ov(c):
            return ps[:, oacc[c][0]:oacc[c][1] + 1, 0:258].rearrange("p b (q c) -> p b q c", c=129)

        def head(h):
            kh, vh = KH[h % 2], VH[h % 2]
            dma("sp", kh.ap, KT[h][:, 0:Sq], [b_KT[h]], [kh])
            dma("sp", vh.ap, VB[h][:, 0:NT, :], [b_VB], [vh])

            def qgroup(qg):
                qi = nxt("qh") % 2
                qps = [QP[0][qi], QP[1][qi]]
                for c_ in range(2):
                    dma("sp", qps[c_].ap[64 * c_:64 * c_ + 64, :], QT[h][64 * c_:64 * c_ + 64, qg * 512:(qg + 1) * 512],
                        [(b_QT[h], qg)], [(qps[c_], "d")])
                qh = qps[0]
                if qg == 0 and NWARM:
                    def warm(e):
                        for i_ in range(NWARM):
                            ins = e.matmul(ps[:, i_ % 2, :], lhsT=IDB, rhs=HT[0].ap[:, 0, :], start=True, stop=True)
                        return ins
                    S.add("pe", warm, R=[CSTB, HT[0], kh, vh, qh], W=[PB[0], PB[1]])

                def cmap(c):
                    ob = oacc[c]

                    def qk(p, c=c):
                        sc = scp[nxt("scb") % 3]
                        pt = PT[nxt("ptb") % 4]

                        def mm(e):
                            for kk in range(2):
                                kt = 2 * p + kk
                                ins = e.matmul(ps[:, sc[kk], :], lhsT=kh.ap[:, kt * 128:(kt + 1) * 128],
                                               rhs=qps[c].ap, start=True, stop=True)
                            return ins
                        S.add("pe", mm, R=[kh, qps[c]], W=[PB[sc[0]], PB[sc[1]]])
                        S.add("act", lambda e: e.activation(out=pt.ap, in_=ps[:, sc[0]:sc[1] + 1, :], func=AF.Exp,
                                                            scale=0.125), W=[pt], X=[PB[sc[0]], PB[sc[1]]])
                        return pt

                    def pv(p, pt, ob=ob):
                        def mm(e):
                            for kk in range(2):
                                kt = 2 * p + kk
                                for qt in range(4):
                                    ins = e.matmul(ps[:, ob[qt // 2], (qt % 2) * 129:(qt % 2) * 129 + 129],
                                                   lhsT=pt.ap[:, kk, qt * 128:(qt + 1) * 128], rhs=vh.ap[:, kt, :],
                                                   start=(p == 0 and kk == 0 and qt % 2 == 0),
                                                   stop=(p == NP - 1 and kk == 1), skip_group_check=True)
                            return ins
                        S.add("pe", mm, R=[pt, vh], W=[PB[ob[0]], PB[ob[1]]])

                    def step_pv(p, pt):
                        pv(p, pt)
                        if p == NP - 1:
                            round_end_of(c, ob)

                    for p in range(NP):
                        push(lambda p=p: qk(p), lambda pt, p=p: step_pv(p, pt))

                def round_end_of(c, ob):
                    rc = RC[nxt("rc") % 4]
                    pbs = [PB[ob[0]], PB[ob[1]]]
                    rcv = rc.ap[:, 0:4].rearrange("p (b q) -> p b q", b=2)
                    if c == 0:
                        S.add("dve", lambda e, rcv=rcv: e.reciprocal(out=rcv, in_=ov(0)[:, :, :, 128]),
                              W=[(rc, 0)], X=pbs)
                        S.add("dve", lambda e, rcv=rcv: e.tensor_tensor(
                            out=O1S.ap.rearrange("p (b q) c -> p b q c", b=2), in0=ov(0)[:, :, :, 0:128],
                            in1=rcv.unsqueeze(3).to_broadcast([128, 2, 2, 128]), op=ALU.mult),
                            R=[(rc, 0)], W=[O1S], X=pbs)
                    else:
                        S.add("dve", lambda e, rcv=rcv: e.reciprocal(out=rcv, in_=ov(1)[:, :, :, 128]),
                              W=[(rc, 0)], X=pbs)
                        S.add("dve", lambda e, rc=rc: e.tensor_scalar(
                            out=rc.ap[:, 4:8], in0=rc.ap[:, 0:4], scalar1=NEGLAM.ap[:, jb:jb + 1], scalar2=None,
                            op0=ALU.mult), R=[(rc, 0), NEGLAM], W=[(rc, 1)])
                        S.add("dve", lambda e, rc=rc: e.tensor_tensor(
                            out=OS.ap.rearrange("p (b q) c -> p b q c", b=2), in0=ov(1)[:, :, :, 0:128],
                            in1=rc.ap[:, 4:8].rearrange("p (b q) -> p b q", b=2).unsqueeze(3).to_broadcast(
                                [128, 2, 2, 128]), op=ALU.mult), R=[(rc, 1)], W=[OS], X=pbs)
                        S.add("dve", lambda e: e.tensor_tensor(out=OS.ap, in0=OS.ap, in1=O1S.ap, op=ALU.add),
                              R=[O1S], W=[OS])
                        st = STAT[nxt("st") % 16]
                        st2 = STAT[nxt("st") % 16]
                        for qt in range(4):
                            jk = JK[nxt("jk") % 3]
                            S.add("act", lambda e, qt=qt, st=st, jk=jk: e.activation(
                                out=jk.ap[:, 0:128], in_=OS.ap[:, qt, :], func=AF.Square,
                                accum_out=st.ap[:, qt:qt + 1]), R=[OS], W=[(st, qt), jk])
                        S.add("act", lambda e, st=st: e.activation(out=st.ap[:, 4:8], in_=st.ap[:, 0:4], func=AF.Ln,
                                                                   scale=1.0 / 128, bias=EPSB.ap[:, 0:1]),
                              R=[(st, q_) for q_ in range(4)] + [EPSB], W=[(st, 4)])
                        S.add("act", lambda e, st=st, st2=st2: e.activation(out=st2.ap[:, 0:4], in_=st.ap[:, 4:8],
                                                                            func=AF.Exp, scale=-0.5),
                              R=[(st, 4)], W=[st2])
                        for qt in range(4):
                            tile = 4 * qg + qt
                            S.add("dve", lambda e, qt=qt, tile=tile, st2=st2: e.scalar_tensor_tensor(
                                out=OSB.ap[:, tile, h * 128:(h + 1) * 128], in0=OS.ap[:, qt, :],
                                scalar=st2.ap[:, qt:qt + 1], in1=SUBG.ap[:, jb, :], op0=ALU.mult, op1=ALU.mult),
                                R=[OS, st2, SUBG], W=[(OSB, tile)])

                for c in range(2):
                    cmap(c)

            for qg in range(NG):
                qgroup(qg)

        for h in range(8):
            head(h)
        flush()
        S.phase_switch()
        A.reg_top = reg_mark
        P3(sq, l, OSB)

    def P4(sq, l):
        Sq = sq["S"]
        NG = Sq // 512
        A.reset_region()
        S.phase_switch()
        WD = A.reg("wd", [NFF, D], BF16)
        AT = A.reg("at", [NFF, 512], BF16)
        SG = [A.reg(f"sg{i}", [512], F32) for i in range(2)]
        dma("sp", WD.ap, WDN[l].rearrange("(kc p) n -> p kc n", p=128), [b_WDN[l]], [WD])
        load_post(GPF, g_ffn_post, l)
        gub = [(2, 3), (4, 5)]
        dnb = [(6, 7), (4, 5)]

        def gate_up(g):
            ht = HT[g % 2]
            for j in range(NFF):
                ws = WS[nxt("ws") % 4]
                dma("sp", ws.ap, WGU[l][j], [b_WGU[l]], [ws])
                bk = gub[nxt("gub") % 2]
                sg = SG[nxt("sg") % 2]

                def mm(e, ws=ws, bk=bk):
                    for hf in range(2):
                        for kc in range(KC):
                            i = e.matmul(ps[:, bk[hf], :], lhsT=ws.ap[:, kc, hf * 128:(hf + 1) * 128],
                                         rhs=ht.ap[:, kc, :], start=(kc == 0), stop=(kc == KC - 1))
                    return i
                S.add("pe", mm, R=[ws, ht], W=[PB[bk[0]], PB[bk[1]]])
                S.add("act", lambda e, sg=sg, bk=bk: e.activation(out=sg.ap, in_=ps[:, bk[0], :], func=AF.Silu),
                      W=[sg], X=[PB[bk[0]]])
                S.add("dve", lambda e, sg=sg, bk=bk, j=j: e.tensor_tensor(out=AT.ap[:, j, :], in0=sg.ap,
                                                                          in1=ps[:, bk[1], :], op=ALU.mult),
                      R=[sg], W=[(AT, j)], X=[PB[bk[1]]])

        def down(g):
            for t in range(4):
                tile = 4 * g + t
                bk = dnb[nxt("dnb") % 2]

                def mm(e, t=t, bk=bk):
                    for hf in range(2):
                        for kc in range(NFF):
                            i = e.matmul(ps[:, bk[hf], :], lhsT=AT.ap[:, kc, t * 128:(t + 1) * 128],
                                         rhs=WD.ap[:, kc, hf * 512:(hf + 1) * 512],
                                         start=(kc == 0), stop=(kc == NFF - 1))
                    return i
                S.add("pe", mm, R=[AT, WD], W=[PB[bk[0]], PB[bk[1]]])
                post_tile(sq, l, tile, bk, GPF, True)

        front(sq, l, 1, 0, True)
        for g in range(NG):
            gate_up(g)
            if g + 1 < NG:
                front(sq, l, 1, g + 1, True)
            down(g)

    stop = cfg.get("stop", 99)
    prep()
    nph = 0
    for sq in seqs:
        for l in range(DEPTH):
            for ph in (P1, P2A if l % 2 == 0 else P2B, P4):
                nph += 1
                if nph <= stop:
                    ph(sq, l)
    S.add("sp", None, R=[b for sq in seqs for b in sq["yb"]])
    S.finalize()

    sems = {}
    for e_ in ("pe", "act", "dve", "pool"):
        sems[e_] = nc.alloc_semaphore(name="s_" + e_)
    for q_ in ("sp", "pool"):
        for i in range(NDMA_SEM):
            sems[(q_, i)] = nc.alloc_semaphore(name=f"d_{q_}{i}")
    nw = {}
    with nc.Block() as block:
        @block.sync
        def _(e):
            nw["sp"] = S.emit("sp", e, sems)

        @block.tensor
        def _(e):
            nw["pe"] = S.emit("pe", e, sems)

        @block.scalar
        def _(e):
            nw["act"] = S.emit("act", e, sems)

        @block.vector
        def _(e):
            nw["dve"] = S.emit("dve", e, sems)

        @block.gpsimd
        def _(e):
            nw["pool"] = S.emit("pool", e, sems)
    if cfg.get("verbose"):
        print("ops", {e: len(S.q[e]) for e in ENGS}, "waits", nw, "arena top", A.shared_top)
    return nc


def make_consts(smax):
    inv = (1.0 / (10000.0 ** (np.arange(0, 64, 2, dtype=np.float32) / np.float32(64)))).astype(np.float32)
    ang = np.arange(smax, dtype=np.float32)[:, None] * inv[None, :]
    cos = np.cos(ang).astype(np.float32)
    sin = np.sin(ang).astype(np.float32)
    idx = (np.arange(128) % 64) % 32
    cos_t = np.ascontiguousarray(cos[:, idx].T)
    sin_t = np.ascontiguousarray(sin[:, idx].T)
    cst = np.zeros((128, 4, 128), np.float32)
    cst[:, 0, :] = np.eye(128, dtype=np.float32)
    for m in range(128):
        if (m % 64) < 32:
            cst[m + 32, 1, m] = -1.0
        else:
            cst[m - 32, 1, m] = 1.0
    k = np.arange(128)[:, None]
    q = np.arange(128)[None, :]
    cst[:, 2, :] = (k >= q).astype(np.float32)
    cst[:, 3, :] = (k <= q).astype(np.float32)
    return cos_t, sin_t, cst


W_NAMES = ["norm_mix_pre", "norm_mix_post", "norm_ffn_pre", "norm_ffn_post", "a_w_in", "a_w_o", "a_sinks",
           "b_w_in", "b_w_o", "b_lambda_q1", "b_lambda_k1", "b_lambda_q2", "b_lambda_k2", "b_subln",
           "ffn_w_gate_up", "ffn_w_down"]

_CACHE = {}


def run(inputs, n_cores=8, verbose=False, trace=False, stop=99):
    xp = np.asarray(inputs["x_prompt"], np.float32)
    xs = np.asarray(inputs["x_sample"], np.float32)
    depth = inputs["norm_mix_pre"].shape[0]
    npc, nsc = xp.shape[0] // n_cores, xs.shape[0] // n_cores
    cfg = dict(depth=depth, seqs=[(xp.shape[1], npc), (xs.shape[1], nsc)], verbose=verbose, stop=stop)
    key = (depth, xp.shape, xs.shape, stop)
    if key not in _CACHE:
        _CACHE[key] = build(cfg)
    nc = _CACHE[key]
    smax = max(xp.shape[1], xs.shape[1])
    cos_t, sin_t, cst = make_consts(smax)
    shared = {k: np.ascontiguousarray(np.asarray(inputs[k], np.float32)) for k in W_NAMES
              if not (depth < 2 and k.startswith("b_"))}
    shared.update(cos_t=cos_t, sin_t=sin_t, cst=cst)
    in_maps = []
    for c in range(n_cores):
        m = dict(shared)
        m["x0"] = np.ascontiguousarray(xp[c * npc:(c + 1) * npc])
        m["x1"] = np.ascontiguousarray(xs[c * nsc:(c + 1) * nsc])
        in_maps.append(m)
    res = run_bass_kernel_spmd(nc, in_maps, core_ids=list(range(n_cores)), trace=trace)
    yp = np.concatenate([res.results[c]["y0"] for c in range(n_cores)], axis=0).astype(np.float32)
    ys = np.concatenate([res.results[c]["y1"] for c in range(n_cores)], axis=0).astype(np.float32)
    return (yp, ys), res


def kernel(**inputs):
    out, _ = run(inputs)
    return out
```

```python
import math
import numpy as np
import concourse.bass as bass
import concourse.mybir as mybir
from concourse.bass_utils import run_bass_kernel_spmd

F32 = mybir.dt.float32
BF16 = mybir.dt.bfloat16
AF = mybir.ActivationFunctionType
ALU = mybir.AluOpType
AX = mybir.AxisListType

D = 1024
KC = 8
DFF = 2816
NFF = 22
EPS = 1e-6
NDMA_SEM = 40
import os
DBG = int(os.environ.get('KDBG', '9'))
KV = int(os.environ.get('KV', '0'))
NWARM = int(os.environ.get('NWARM', '0'))
ENGS = ("pe", "act", "dve", "pool", "sp")


class Buf:
    __slots__ = ("name", "w", "r", "region")

    def __init__(self, name, region=False):
        self.name = name
        self.w = {}
        self.r = {}
        self.region = region


class Op:
    __slots__ = ("eng", "fn", "deps", "flag", "cnt", "dma", "dsem", "dval", "xb")


class Sched:
    def __init__(self):
        self.q = {e: [] for e in ENGS}
        self.ndma = {e: 0 for e in ENGS}
        self.dma_hist = {e: [] for e in ENGS}
        self.reg_eng = {}
        self.reg_dma = {}
        self.barrier = []
        self.barrier_done = set(ENGS)
        self.nops = 0

    @staticmethod
    def _norm(items):
        out = []
        for it in items:
            if isinstance(it, tuple):
                out.append(it)
            else:
                out.append((it, None))
        return out

    def phase_switch(self):
        self.barrier = list(self.reg_eng.values()) + list(self.reg_dma.values())
        self.barrier_done = set()

    def add(self, eng, fn, R=(), W=(), X=(), dma=False):
        op = Op()
        op.eng, op.fn, op.dma, op.flag, op.cnt = eng, fn, dma, False, 0
        op.dsem = op.dval = None
        deps = []
        R = self._norm(R)
        X = self._norm(X)
        op.xb = set(id(b) for b, _ in X)
        W = self._norm(W) + X
        touches_region = False
        for b, t in R:
            touches_region |= b.region
            if t is None:
                for w in b.w.values():
                    deps.append((w, True))
            else:
                w = b.w.get(t)
                if w is not None:
                    deps.append((w, True))
                w = b.w.get(None)
                if w is not None:
                    deps.append((w, True))
        for b, t in W:
            touches_region |= b.region
            isx = id(b) in op.xb
            if t is None:
                for w in b.w.values():
                    deps.append((w, 2 if (isx and id(b) in w.xb) else False))
                for rd in b.r.values():
                    for r in rd.values():
                        deps.append((r, False))
            else:
                for tt in (t, None):
                    w = b.w.get(tt)
                    if w is not None:
                        deps.append((w, 2 if (isx and id(b) in w.xb) else False))
                    rd = b.r.get(tt)
                    if rd:
                        for r in rd.values():
                            deps.append((r, False))
        if touches_region and eng not in self.barrier_done:
            self.barrier_done.add(eng)
            for d in self.barrier:
                deps.append((d, False))
        if dma:
            k = self.ndma[eng]
            op.dsem = (eng, k % NDMA_SEM)
        rkey = op.dsem if dma else eng
        for b, t in R:
            b.r.setdefault(t, {})[rkey] = op
        for b, t in W:
            if t is None:
                b.w = {None: op}
                b.r = {}
            else:
                b.w[t] = op
                b.r[t] = {}
        if dma:
            k = self.ndma[eng]
            self.ndma[eng] = k + 1
            op.dsem = (eng, k % NDMA_SEM)
            op.dval = 16 * (k // NDMA_SEM + 1)
            hist = self.dma_hist[eng]
            if k >= NDMA_SEM:
                deps.append((hist[k - NDMA_SEM], False))
            hist.append(op)
            if touches_region:
                self.reg_dma[op.dsem] = op
        elif touches_region:
            self.reg_eng[eng] = op
        op.deps = deps
        self.q[eng].append(op)
        self.nops += 1
        return op

    def finalize(self):
        for e in ENGS:
            for op in self.q[e]:
                for d, raw in op.deps:
                    if d.dma:
                        continue
                    if d.eng == e and (e == "pe" or raw == 2):
                        continue
                    d.flag = True
        for e in ENGS:
            c = 0
            for op in self.q[e]:
                if op.dma:
                    continue
                if op.flag:
                    c += 1
                    op.cnt = c

    def emit(self, eng, handle, sems):
        waited = {}
        nw = 0
        for op in self.q[eng]:
            for d, raw in op.deps:
                if d.dma:
                    key, val = d.dsem, d.dval
                else:
                    if d.eng == eng and (eng == "pe" or raw == 2):
                        continue
                    key, val = d.eng, d.cnt
                if waited.get(key, 0) >= val:
                    continue
                handle.wait_ge(sems[key], val)
                waited[key] = val
                nw += 1
            if op.fn is None:
                continue
            inst = op.fn(handle)
            if op.dma:
                inst.then_inc(sems[op.dsem], 16)
            elif op.flag:
                inst.then_inc(sems[eng], 1)
        return nw


class SB(Buf):
    __slots__ = ("ap", "nbytes")

    def __init__(self, name, ap, nbytes, region=False):
        Buf.__init__(self, name, region)
        self.ap = ap
        self.nbytes = nbytes


class Arena:
    def __init__(self, nc, nbytes):
        self.h = nc.alloc_sbuf_tensor("arena", [128, nbytes // 2], BF16)
        self.ap = self.h.ap()
        self.nbytes = nbytes
        self.shared_top = 0
        self.reg_top = 0

    def _view(self, off, shape, dt):
        n = 1
        for s in shape:
            n *= s
        esz = 4 if dt == F32 else 2
        nb = n * esz
        nb_al = (nb + 31) // 32 * 32
        assert off + nb_al <= self.nbytes, ("SBUF arena overflow", off, nb_al, self.nbytes)
        a = self.ap[:, off // 2: off // 2 + nb // 2]
        if dt == F32:
            a = a.bitcast(F32)
        if len(shape) == 2:
            a = a.rearrange("p (a b) -> p a b", a=shape[0])
        elif len(shape) == 3:
            a = a.rearrange("p (a b c) -> p a b c", a=shape[0], b=shape[1])
        elif len(shape) == 4:
            a = a.rearrange("p (a b c d) -> p a b c d", a=shape[0], b=shape[1], c=shape[2])
        return a, nb_al

    def shared(self, name, shape, dt):
        a, nb = self._view(self.shared_top, shape, dt)
        self.shared_top += nb
        self.reg_top = self.shared_top
        return SB(name, a, nb, False)

    def reset_region(self):
        self.reg_top = self.shared_top

    def reg(self, name, shape, dt):
        a, nb = self._view(self.reg_top, shape, dt)
        self.reg_top += nb
        return SB(name, a, nb, True)


def lambda_init_of(layer_idx):
    return 0.8 - 0.6 * math.exp(-0.3 * layer_idx)


def build(cfg):
    DEPTH = cfg["depth"]
    NA = (DEPTH + 1) // 2
    NB = DEPTH // 2
    seqdefs = cfg["seqs"]
    SMAX = max(s for s, _ in seqdefs)
    NTMAX = SMAX // 128

    nc = bass.Bass("TRN2", target_bir_lowering=False)
    S = Sched()

    def din(name, shape):
        return nc.dram_tensor(name, list(shape), F32, kind="ExternalInput").ap()

    xin = [din(f"x{i}", (n, s, D)) for i, (s, n) in enumerate(seqdefs)]
    yout = [nc.dram_tensor(f"y{i}", [n, s, D], F32, kind="ExternalOutput").ap()
            for i, (s, n) in enumerate(seqdefs)]
    g_mix_pre = din("norm_mix_pre", (DEPTH, D))
    g_mix_post = din("norm_mix_post", (DEPTH, D))
    g_ffn_pre = din("norm_ffn_pre", (DEPTH, D))
    g_ffn_post = din("norm_ffn_post", (DEPTH, D))
    a_w_in = din("a_w_in", (NA, D, 1536))
    a_w_o = din("a_w_o", (NA, D, D))
    a_sinks = din("a_sinks", (NA, 16))
    if NB:
        b_w_in = din("b_w_in", (NB, D, 3072))
        b_w_o = din("b_w_o", (NB, D, D))
        b_lq1 = din("b_lambda_q1", (NB, 64))
        b_lk1 = din("b_lambda_k1", (NB, 64))
        b_lq2 = din("b_lambda_q2", (NB, 64))
        b_lk2 = din("b_lambda_k2", (NB, 64))
        b_subln = din("b_subln", (NB, 128))
    w_gu = din("ffn_w_gate_up", (DEPTH, D, 2 * DFF))
    w_dn = din("ffn_w_down", (DEPTH, DFF, D))
    cosT = din("cos_t", (128, SMAX))
    sinT = din("sin_t", (128, SMAX))
    cst = din("cst", (128, 8, 128))

    def scr(name, shape):
        return nc.dram_tensor(name, list(shape), BF16, kind="Internal").ap()

    WINA = [scr(f"wina{l}", (7, 128, KC, 256)) for l in range(NA)]
    WOA = [scr(f"woa{l}", (D, D)) for l in range(NA)]
    WINB = [scr(f"winb{l}", (12, 128, KC, 256)) for l in range(NB)]
    WOB = [scr(f"wob{l}", (D, D)) for l in range(NB)]
    WGU = [scr(f"wgu{i}", (NFF, 128, KC, 256)) for i in range(DEPTH)]
    WDN = [scr(f"wdn{i}", (DFF, D)) for i in range(DEPTH)]
    QT = scr("qt", (8, 128, SMAX))
    KT = scr("kt", (8, 128, SMAX))
    VB = scr("vb", (8, 128, NTMAX, 129))
    VA = scr("va", (128, NTMAX, 4, 65))

    b_WINA = [Buf(f"wina{l}") for l in range(NA)]
    b_WOA = [Buf(f"woa{l}") for l in range(NA)]
    b_WINB = [Buf(f"winb{l}") for l in range(NB)]
    b_WOB = [Buf(f"wob{l}") for l in range(NB)]
    b_WGU = [Buf(f"wgu{i}") for i in range(DEPTH)]
    b_WDN = [Buf(f"wdn{i}") for i in range(DEPTH)]
    b_QT = [Buf(f"qt{c}") for c in range(8)]
    b_KT = [Buf(f"kt{c}") for c in range(8)]
    b_VB = Buf("vb")
    b_VA = Buf("va")
    b_IN = Buf("inputs")

    seqs = []
    for i, (s, n) in enumerate(seqdefs):
        for k in range(n):
            seqs.append(dict(S=s, xin=xin[i][k], y=yout[i][k],
                             yb=[Buf(f"y{i}_{k}_{t}") for t in range(s // 128)]))

    A = Arena(nc, cfg.get("arena", 204 * 1024))
    ps = nc.alloc_psum_tensor("ps", [128, 8, 512], F32).ap()
    PB = [Buf(f"psum{b}") for b in range(8)]

    def trv(b):
        return ps[:, b, :].bitcast(BF16).rearrange("p (k t) -> p k t", k=8)

    CSTB = A.shared("cstb", [8, 128], BF16)
    IDB = CSTB.ap[:, 0, :]
    RMB = CSTB.ap[:, 1, :]
    MKP = CSTB.ap[:, 2, :]
    MKN = CSTB.ap[:, 3, :]
    GPRE = A.shared("gpre", [2, DEPTH, 8], F32)
    ESK = A.shared("esk", [NA, 4, 4], F32)
    NEGLAM = A.shared("neglam", [max(NB, 1)], F32)
    SUBG = A.shared("subg", [max(NB, 1), 128], F32)
    EPSB = A.shared("epsb", [8], F32)
    GPM = A.shared("gpm", [D], F32)
    GPF = A.shared("gpf", [D], F32)
    XT = [A.shared(f"xt{i}", [D], F32) for i in range(4)]
    JK = [A.shared(f"junk{i}", [128], BF16) for i in range(3)]
    HT = [A.shared(f"ht{i}", [KC, 512], BF16) for i in range(2)]
    STAT = [A.shared(f"stat{i}", [8], F32) for i in range(16)]
    WS = [A.shared(f"ws{i}", [KC, 256], BF16) for i in range(4)]
    WO = A.shared("wo", [KC, D], BF16)
    TMP = [A.shared(f"tmp{i}", [D], F32) for i in range(2)]

    ctr = {}

    def nxt(k):
        v = ctr.get(k, 0)
        ctr[k] = v + 1
        return v

    def dma(q, out, in_, R, W, **kw):
        return S.add(q, lambda e, o=out, i=in_, kw=kw: e.dma_start(out=o, in_=i, **kw), R=R, W=W, dma=True)

    def layer_kind(l):
        return ("A", l // 2) if l % 2 == 0 else ("B", l // 2)

    def prep():
        A.reset_region()
        S.phase_switch()
        F = [A.reg(f"pf{i}", [3072], F32) for i in range(2)]
        Bq = [A.reg(f"pb{i}", [3072], BF16) for i in range(2)]
        SM = A.reg("psm", [1024], F32)
        dma("sp", F[0].ap[:, 0:1024], cst.rearrange("p a b -> p (a b)"), [b_IN], [F[0]])
        S.add("dve", lambda e: e.tensor_copy(out=CSTB.ap.rearrange("p a b -> p (a b)"), in_=F[0].ap[:, 0:1024]),
              R=[F[0]], W=[CSTB])
        S.add("dve", lambda e: e.memset(EPSB.ap, EPS), W=[EPSB])
        dma("sp", GPRE.ap[:, 0, :, :], g_mix_pre.rearrange("l (kc p) -> p l kc", p=128), [b_IN], [(GPRE, 0)],
            allow_slow_non_contiguous=True)
        dma("sp", GPRE.ap[:, 1, :, :], g_ffn_pre.rearrange("l (kc p) -> p l kc", p=128), [b_IN], [(GPRE, 1)],
            allow_slow_non_contiguous=True)
        for la in range(NA):
            src = a_sinks[la:la + 1, :].partition_broadcast(128)[:, 0, :]
            dma("sp", SM.ap[:, la * 16:(la + 1) * 16], src, [b_IN], [(SM, la)])
        S.add("act", lambda e: e.activation(out=ESK.ap.rearrange("p a b c -> p (a b c)"), in_=SM.ap[:, 0:NA * 16],
                                            func=AF.Exp), R=[SM], W=[ESK])
        if NB:
            o = 64
            for nm, src in (("q1", b_lq1), ("k1", b_lk1), ("q2", b_lq2), ("k2", b_lk2)):
                s2 = src.rearrange("(o a) b -> o (a b)", o=1).partition_broadcast(128)[:, 0, :]
                dma("sp", SM.ap[:, o:o + NB * 64], s2, [b_IN], [(SM, nm)])
                o += NB * 64
            sg = b_subln.rearrange("(o a) b -> o (a b)", o=1).partition_broadcast(128)[:, 0, :]
            dma("sp", SM.ap[:, o:o + NB * 128], sg, [b_IN], [(SM, "sg")])
            osg = o
            q1 = SM.ap[:, 64:64 + NB * 64]
            k1 = SM.ap[:, 64 + NB * 64:64 + 2 * NB * 64]
            q2 = SM.ap[:, 64 + 2 * NB * 64:64 + 3 * NB * 64]
            k2 = SM.ap[:, 64 + 3 * NB * 64:64 + 4 * NB * 64]
            pr = A.reg("ppr", [2, NB, 64], F32)
            rd = A.reg("prd", [2, NB], F32)
            ex = A.reg("pex", [2, NB], F32)
            S.add("dve", lambda e: e.tensor_tensor(out=pr.ap[:, 0].rearrange("p a b -> p (a b)"), in0=q1, in1=k1,
                                                   op=ALU.mult), R=[SM], W=[(pr, 0)])
            S.add("dve", lambda e: e.tensor_tensor(out=pr.ap[:, 1].rearrange("p a b -> p (a b)"), in0=q2, in1=k2,
                                                   op=ALU.mult), R=[SM], W=[(pr, 1)])
            S.add("dve", lambda e: e.tensor_reduce(out=rd.ap.rearrange("p a b -> p (a b)"),
                                                   in_=pr.ap.rearrange("p a b c -> p (a b) c"),
                                                   axis=AX.X, op=ALU.add), R=[pr], W=[rd])
            S.add("act", lambda e: e.activation(out=ex.ap.rearrange("p a b -> p (a b)"),
                                                in_=rd.ap.rearrange("p a b -> p (a b)"), func=AF.Exp),
                  R=[rd], W=[ex])
            S.add("dve", lambda e: e.tensor_tensor(out=rd.ap[:, 0, :], in0=ex.ap[:, 1, :], in1=ex.ap[:, 0, :],
                                                   op=ALU.subtract), R=[ex], W=[rd])
            for j in range(NB):
                li = lambda_init_of(2 * j + 1)
                S.add("dve", lambda e, j=j, li=li: e.tensor_scalar_add(out=NEGLAM.ap[:, j:j + 1],
                                                                       in0=rd.ap[:, 0, j:j + 1], scalar1=-li),
                      R=[rd], W=[(NEGLAM, j)])
                S.add("dve", lambda e, j=j, li=li: e.tensor_scalar_mul(
                    out=SUBG.ap[:, j, :], in0=SM.ap[:, osg + j * 128: osg + (j + 1) * 128], scalar1=(1.0 - li)),
                    R=[SM], W=[(SUBG, j)])

        it = [0]

        def conv(src_ap, ncols, stores, wbuf):
            i = it[0]
            it[0] += 1
            f, b = F[i % 2], Bq[i % 2]
            dma("sp", f.ap[:, 0:ncols], src_ap, [b_IN], [f])
            if i % 2 == 0:
                S.add("dve", lambda e: e.tensor_copy(out=b.ap[:, 0:ncols], in_=f.ap[:, 0:ncols]), R=[f], W=[b])
            else:
                S.add("act", lambda e: e.activation(out=b.ap[:, 0:ncols], in_=f.ap[:, 0:ncols], func=AF.Copy),
                      R=[f], W=[b])
            for dst, srcv in stores(b.ap):
                dma("pool", dst, srcv, [b], [(wbuf, nxt("wtag"))])

        for l in range(NA):
            for kc in range(KC):
                def st(bap, l=l, kc=kc):
                    out = []
                    out.append((WINA[l][0:4, :, kc, :].rearrange("s p c -> p s c"),
                                bap[:, 0:1024].rearrange("p (s c) -> p s c", s=4)))
                    out.append((WINA[l][6, :, kc, :], bap[:, 1280:1536]))
                    for s_ in range(2):
                        for d in range(2):
                            dst = WINA[l][4 + s_, :, kc, :].rearrange("p (k d c) -> p k d c", k=2, d=2)[:, :, d, :]
                            out.append((dst, bap[:, 1024 + s_ * 128:1152 + s_ * 128].rearrange(
                                "p (k c) -> p k c", k=2)))
                    return out
                conv(a_w_in[l, kc * 128:(kc + 1) * 128, :], 1536, st, b_WINA[l])
            for kc in range(KC):
                conv(a_w_o[l, kc * 128:(kc + 1) * 128, :], D,
                     lambda bap, l=l, kc=kc: [(WOA[l][kc * 128:(kc + 1) * 128, :], bap[:, 0:D])], b_WOA[l])
        for l in range(NB):
            for kc in range(KC):
                conv(b_w_in[l, kc * 128:(kc + 1) * 128, :], 3072,
                     lambda bap, l=l, kc=kc: [(WINB[l][:, :, kc, :].rearrange("s p c -> p s c"),
                                               bap[:, 0:3072].rearrange("p (s c) -> p s c", s=12))], b_WINB[l])
            for kc in range(KC):
                conv(b_w_o[l, kc * 128:(kc + 1) * 128, :], D,
                     lambda bap, l=l, kc=kc: [(WOB[l][kc * 128:(kc + 1) * 128, :], bap[:, 0:D])], b_WOB[l])
        for i in range(DEPTH):
            for kc in range(KC):
                for half in range(2):
                    conv(w_gu[i, kc * 128:(kc + 1) * 128, half * DFF:(half + 1) * DFF], DFF,
                         lambda bap, i=i, kc=kc, half=half: [
                             (WGU[i][:, :, kc, half * 128:(half + 1) * 128].rearrange("j p c -> p j c"),
                              bap[:, 0:DFF].rearrange("p (j c) -> p j c", j=NFF))], b_WGU[i])
            for rc in range(NFF):
                conv(w_dn[i, rc * 128:(rc + 1) * 128, :], D,
                     lambda bap, i=i, rc=rc: [(WDN[i][rc * 128:(rc + 1) * 128, :], bap[:, 0:D])], b_WDN[i])

    def xsrc(sq, l, tile, mid):
        if l == 0 and not mid:
            return sq["xin"][tile * 128:(tile + 1) * 128, :], b_IN
        return sq["y"][tile * 128:(tile + 1) * 128, :], sq["yb"][tile]

    def rstd_ops(st, n, dim):
        S.add("act", lambda e: e.activation(out=st.ap[:, n:2 * n], in_=st.ap[:, 0:n], func=AF.Ln,
                                            scale=1.0 / dim, bias=EPSB.ap[:, 0:1]),
              R=[(st, 0), EPSB], W=[(st, 1)])
        S.add("act", lambda e: e.activation(out=st.ap[:, 2 * n:3 * n], in_=st.ap[:, n:2 * n], func=AF.Exp,
                                            scale=-0.5), R=[(st, 1)], W=[(st, 2)])

    def front_a(sq, l, g, mid, xns):
        for t in range(4):
            tile = 4 * g + t
            xt = XT[nxt("xt") % 4]
            st = STAT[nxt("st") % 16]
            xn = xns[t]
            src, sb = xsrc(sq, l, tile, mid)
            dma("sp", xt.ap, src, [sb], [xt])
            S.add("act", lambda e, xt=xt, st=st, xn=xn: e.activation(out=xn.ap, in_=xt.ap, func=AF.Square,
                                                                     accum_out=st.ap[:, 0:1]),
                  R=[xt], W=[(st, 0), xn])
            rstd_ops(st, 1, D)
            S.add("dve", lambda e, xt=xt, st=st, xn=xn: e.tensor_scalar(
                out=xn.ap, in0=xt.ap, scalar1=st.ap[:, 2:3], scalar2=None, op0=ALU.mult),
                R=[xt, (st, 2)], W=[xn])

    def front_b(l, which, g, xns):
        ht = HT[g % 2]
        for t in range(4):
            xn = xns[t]
            tb = nxt("tr") % 2

            def tr(e, xn=xn, tb=tb):
                for kc in range(KC):
                    i = e.transpose(trv(tb)[:, kc, :], xn.ap[:, kc * 128:(kc + 1) * 128], IDB)
                return i
            S.add("pe", tr, R=[xn, CSTB], W=[PB[tb]])
            S.add("dve", lambda e, ht=ht, t=t, tb=tb: e.tensor_tensor(
                out=ht.ap[:, :, t * 128:(t + 1) * 128], in0=trv(tb),
                in1=GPRE.ap[:, which, l, :].unsqueeze(2).to_broadcast([128, KC, 128]), op=ALU.mult),
                R=[GPRE], W=[(ht, t)], X=[PB[tb]])

    def P1(sq, l):
        kind, j = layer_kind(l)
        Sq = sq["S"]
        NG = Sq // 512
        A.reset_region()
        S.phase_switch()
        TC = [A.reg(f"tc{i}", [512], F32) for i in range(2)]
        TS = [A.reg(f"ts{i}", [512], F32) for i in range(2)]
        QR = [A.reg(f"qr{i}", [512], BF16) for i in range(3)]
        T1 = [A.reg(f"t1{i}", [512], F32) for i in range(2)]
        T2 = [A.reg(f"t2{i}", [512], F32) for i in range(2)]
        QO = [A.reg(f"qo{i}", [512], BF16) for i in range(4)]
        if kind == "A":
            VST = A.reg("vst", [4, 4, 65], BF16)
            WIN, bWIN = WINA[j], b_WINA[j]
            slots = [("q", s) for s in range(4)] + [("k", s) for s in range(2)] + [("v", 0)]
        else:
            VST = A.reg("vst", [8, 4, 129], BF16)
            WIN, bWIN = WINB[j], b_WINB[j]
            slots = [("q", s) for s in range(4)] + [("k", s) for s in range(4)] + [("v", s) for s in range(4)]
        S.add("dve", lambda e: e.memset(VST.ap, 1.0), W=[VST])
        XNS = [A.reg(f"xn{i}", [D], BF16) for i in range(8)]

        def xset(g):
            return XNS[(g % 2) * 4:(g % 2) * 4 + 4]

        def proj(g):
            ht = HT[g % 2]
            tc, ts = TC[g % 2], TS[g % 2]
            dma("sp", tc.ap, cosT[:, g * 512:(g + 1) * 512], [b_IN], [tc])
            dma("sp", ts.ap, sinT[:, g * 512:(g + 1) * 512], [b_IN], [ts])
            pend = []

            def second(ci):
                (qr, pj, dstT, dbuf, c) = ci
                rb = 5 + nxt("rot") % 3
                S.add("pe", lambda e: e.matmul(ps[:, rb, :], lhsT=RMB, rhs=qr.ap, start=True, stop=True),
                      R=[qr, CSTB], W=[PB[rb]])
                t1 = T1[nxt("t1") % 2]
                t2 = T2[nxt("t2") % 2]
                qo = QO[nxt("qo") % 4]
                S.add("dve", lambda e: e.tensor_tensor(out=t1.ap, in0=ps[:, rb, :], in1=ts.ap, op=ALU.mult),
                      R=[ts], W=[t1], X=[PB[rb]])
                S.add("dve", lambda e: e.tensor_tensor(out=t2.ap, in0=qr.ap, in1=tc.ap, op=ALU.mult),
                      R=[qr, tc], W=[t2])
                S.add("dve", lambda e: e.tensor_tensor(out=qo.ap, in0=t1.ap, in1=t2.ap, op=ALU.add),
                      R=[t1, t2], W=[qo])
                dma("pool", dstT[c][:, g * 512:(g + 1) * 512], qo.ap, [qo], [(dbuf[c], g)])

            for si, (ty, s) in enumerate(slots):
                ws = WS[nxt("ws") % 4]
                dma("sp", ws.ap, WIN[si], [bWIN], [ws])
                if ty in ("q", "k"):
                    for half in range(2):
                        c = 2 * s + half
                        pj = 2 + nxt("pj") % 3
                        qr = QR[nxt("qr") % 3]

                        def mm(e, ws=ws, half=half, pj=pj):
                            for kc in range(KC):
                                i = e.matmul(ps[:, pj, :], lhsT=ws.ap[:, kc, half * 128:(half + 1) * 128],
                                             rhs=ht.ap[:, kc, :], start=(kc == 0), stop=(kc == KC - 1))
                            return i
                        S.add("pe", mm, R=[ws, ht], W=[PB[pj]])
                        S.add("act", lambda e, qr=qr, pj=pj: e.activation(out=qr.ap, in_=ps[:, pj, :], func=AF.Copy),
                              W=[qr], X=[PB[pj]])
                        if pend:
                            second(pend.pop(0))
                        pend.append((qr, pj, QT if ty == "q" else KT, b_QT if ty == "q" else b_KT, c))
                else:
                    while pend:
                        second(pend.pop(0))
                    for t in range(4):
                        pj = 2 + nxt("pj") % 3

                        def mmv(e, ws=ws, t=t, pj=pj):
                            for kc in range(KC):
                                i = e.matmul(ps[:, pj, 0:256], lhsT=ht.ap[:, kc, t * 128:(t + 1) * 128],
                                             rhs=ws.ap[:, kc, :], start=(kc == 0), stop=(kc == KC - 1))
                            return i
                        S.add("pe", mmv, R=[ws, ht], W=[PB[pj]])
                        if kind == "A":
                            o_ap = VST.ap[:, t, :, 0:64]
                            i_ap = ps[:, pj, 0:256].rearrange("p (k c) -> p k c", k=4)
                        else:
                            o_ap = VST.ap[:, 2 * s:2 * s + 2, t, 0:128]
                            i_ap = ps[:, pj, 0:256].rearrange("p (k c) -> p k c", k=2)
                        S.add("act", lambda e, o_ap=o_ap, i_ap=i_ap: e.activation(out=o_ap, in_=i_ap, func=AF.Copy),
                              W=[(VST, (t, s))], X=[PB[pj]])
            while pend:
                second(pend.pop(0))
            if kind == "A":
                dma("pool", VA[:, 4 * g:4 * g + 4, :, :], VST.ap, [VST], [(b_VA, g)])
            else:
                dma("pool", VB[:, :, 4 * g:4 * g + 4, :].rearrange("h p t c -> p h t c"), VST.ap, [VST], [(b_VB, g)])

        front_a(sq, l, 0, False, xset(0))
        front_b(l, 0, 0, xset(0))
        if NG > 1:
            front_a(sq, l, 1, False, xset(1))
        for g in range(NG):
            if g + 2 < NG:
                front_a(sq, l, g + 2, False, xset(g + 2))
            if g + 1 < NG:
                front_b(l, 0, g + 1, xset(g + 1))
            proj(g)

    def load_post(buf, gsrc, l):
        dma("sp", buf.ap, gsrc[l:l + 1, :].partition_broadcast(128)[:, 0, :], [b_IN], [buf])

    def post_tile(sq, l, tile, banks, gp, mid_in):
        b0 = banks[0]
        xt = XT[nxt("xt") % 4]
        st = STAT[nxt("st") % 16]
        tmp = TMP[nxt("tmp") % 2]
        src, sb = xsrc(sq, l, tile, mid_in)
        dma("sp", xt.ap, src, [sb], [xt])
        for hf in range(2):
            S.add("act", lambda e, hf=hf, st=st, tmp=tmp: e.activation(
                out=tmp.ap.bitcast(BF16)[:, hf * 512:(hf + 1) * 512], in_=ps[:, b0 + hf, :], func=AF.Square,
                accum_out=st.ap[:, 3 + hf:4 + hf]), W=[(st, 3 + hf), (tmp, hf)], X=[PB[b0 + hf]])
        S.add("dve", lambda e, st=st: e.tensor_tensor(out=st.ap[:, 0:1], in0=st.ap[:, 3:4], in1=st.ap[:, 4:5],
                                                      op=ALU.add), R=[(st, 3), (st, 4)], W=[(st, 0)])
        rstd_ops(st, 1, D)
        S.add("dve", lambda e, st=st, tmp=tmp: e.scalar_tensor_tensor(
            out=tmp.ap, in0=ps[:, b0:b0 + 2, :].rearrange("p a b -> p (a b)"), scalar=st.ap[:, 2:3], in1=gp.ap,
            op0=ALU.mult, op1=ALU.mult), R=[(st, 2), gp], W=[tmp], X=[PB[b0], PB[b0 + 1]])
        S.add("dve", lambda e, xt=xt, tmp=tmp: e.tensor_tensor(out=tmp.ap, in0=tmp.ap, in1=xt.ap, op=ALU.add),
              R=[xt], W=[tmp])
        dma("pool", sq["y"][tile * 128:(tile + 1) * 128, :], tmp.ap, [tmp], [sq["yb"][tile]])

    def P3(sq, l, OSB):
        NT = sq["S"] // 128
        OT = [A.reg(f"ot{i}", [KC, 128], BF16) for i in range(2)]
        wob = [(2, 3), (4, 5), (6, 7)]

        def trp(tile):
            tb = nxt("tr") % 2
            ot = OT[nxt("ot") % 2]

            def tr(e):
                for kc in range(KC):
                    i = e.transpose(trv(tb)[:, kc, :], OSB.ap[:, tile, kc * 128:(kc + 1) * 128], IDB)
                return i
            S.add("pe", tr, R=[(OSB, tile), CSTB], W=[PB[tb]])
            S.add("act", lambda e: e.activation(out=ot.ap, in_=trv(tb), func=AF.Copy), W=[ot], X=[PB[tb]])
            return ot

        nxt_ot = trp(0)
        for tile in range(NT):
            ot = nxt_ot
            if tile + 1 < NT:
                nxt_ot = trp(tile + 1)
            bk = wob[nxt("wob") % 3]

            def mm(e, ot=ot, bk=bk):
                for hf in range(2):
                    for kc in range(KC):
                        i = e.matmul(ps[:, bk[hf], :], lhsT=ot.ap[:, kc, :], rhs=WO.ap[:, kc, hf * 512:(hf + 1) * 512],
                                     start=(kc == 0), stop=(kc == KC - 1))
                return i
            S.add("pe", mm, R=[ot, WO], W=[PB[bk[0]], PB[bk[1]]])
            post_tile(sq, l, tile, bk, GPM, False)

    def P2A(sq, l):
        kind, la = layer_kind(l)
        Sq = sq["S"]
        NT = Sq // 128
        NG = Sq // 512
        A.reset_region()
        S.phase_switch()
        OSB = A.reg("osb", [NT, D], BF16)
        reg_mark = A.reg_top
        QA = [[A.reg(f"qa{w}_{i}", [8, 512], BF16) for i in range(2)] for w in range(2)]
        for w_ in range(2):
            for qa_ in QA[w_]:
                S.add("dve", lambda e, qa_=qa_: e.memset(qa_.ap, 0.0), W=[qa_])
        KA = [A.reg(f"ka{i}", [4, 768], BF16) for i in range(2)]
        VAs = [A.reg(f"vas{i}", [6, 4, 65], BF16) for i in range(2)]
        PTA = [A.reg(f"pta{i}", [3, 256], BF16) for i in range(3)]
        DEN = [A.reg(f"den{i}", [8], F32) for i in range(4)]
        dma("sp", WO.ap, WOA[la].rearrange("(kc p) n -> p kc n", p=128), [b_WOA[la]], [WO])
        load_post(GPM, g_mix_post, l)
        scsets = [(0, 1, 2), (3, 4, 5)]
        def group(g):
            qas, ka, va = [QA[0][g % 2], QA[1][g % 2]], KA[g % 2], VAs[g % 2]
            tlo, thi = max(0, 4 * g - 1), min(NT - 1, 4 * g + 4)
            rlo = tlo - (4 * g - 1)
            nt = thi - tlo + 1
            for w_ in range(2):
                dma("sp", qas[w_].ap[64 * w_:64 * w_ + 64, :, :],
                    QT[:, 64 * w_:64 * w_ + 64, g * 512:(g + 1) * 512].rearrange("c p s -> p c s"),
                    [(b_QT[c], g) for c in range(8)], [(qas[w_], "d")])
            dma("sp", ka.ap[:, :, rlo * 128:(rlo + nt) * 128],
                KT[0:4, :, tlo * 128:(thi + 1) * 128].rearrange("c p s -> p c s"),
                [b_KT[c] for c in range(4)], [ka])
            dma("sp", va.ap[:, rlo:rlo + nt, :, :], VA[:, tlo:thi + 1, :, :], [b_VA], [va])
            def qblock(qb):
                i = 4 * g + qb
                vb = [b for b in range(3) if 0 <= i - 1 + b < NT]
                b0, b1 = vb[0], vb[-1]

                def qk(j, w):
                    sc = scsets[w]
                    pt = PTA[nxt("pta") % 3]

                    def mm(e):
                        for b in vb:
                            r = qb + b
                            ins = e.matmul(ps[:, sc[b], 0:256],
                                           lhsT=ka.ap[:, j, r * 128:(r + 1) * 128],
                                           rhs=qas[w].ap[:, 2 * j:2 * j + 2, qb * 128:(qb + 1) * 128],
                                           start=True, stop=(b == 1))
                        for b, ci in ((0, 4), (2, 6)):
                            if b in vb:
                                ins = e.matmul(ps[:, sc[b], 0:256], lhsT=IDB, rhs=CSTB.ap[:, ci:ci + 2, :],
                                               start=False, stop=True)
                        return ins
                    S.add("pe", mm, R=[qas[w], ka, CSTB], W=[PB[sc[b]] for b in vb])
                    S.add("act", lambda e: e.activation(out=pt.ap[:, b0:b1 + 1, :],
                                                        in_=ps[:, sc[b0]:sc[b1] + 1, 0:256],
                                                        func=AF.Exp, scale=0.125),
                          W=[pt], X=[PB[sc[b]] for b in vb])
                    return pt

                def pv(j, w, pt):
                    ob = 6 + nxt("oa") % 2

                    def mm(e):
                        first = True
                        for hh in range(2):
                            for b in vb:
                                r = qb + b
                                ins = e.matmul(ps[:, ob, hh * 128:hh * 128 + 65],
                                               lhsT=pt.ap[:, b, hh * 128:(hh + 1) * 128],
                                               rhs=va.ap[:, r, j, :], start=first, stop=(b == vb[-1]),
                                               skip_group_check=True)
                                first = False
                        return ins
                    S.add("pe", mm, R=[pt, va], W=[PB[ob]])
                    den = DEN[nxt("den") % 4]
                    oav = ps[:, ob, 0:256].rearrange("p (hh c) -> p hh c", hh=2)
                    S.add("dve", lambda e: e.tensor_tensor(
                        out=den.ap[:, 0:2], in0=oav[:, :, 64],
                        in1=ESK.ap[:, la, j, :].rearrange("p (hh w) -> p w hh", hh=2)[:, w, :], op=ALU.add),
                        R=[ESK], W=[(den, 0)], X=[PB[ob]])
                    S.add("dve", lambda e: e.reciprocal(out=den.ap[:, 4:6], in_=den.ap[:, 0:2]),
                          R=[(den, 0)], W=[(den, 1)])
                    h0 = 4 * j + w
                    oo = OSB.ap[:, i, :].rearrange("p (h e) -> p h e", e=64)[:, h0:h0 + 3:2, :]
                    S.add("dve", lambda e: e.tensor_tensor(
                        out=oo, in0=oav[:, :, 0:64],
                        in1=den.ap[:, 4:6].unsqueeze(2).to_broadcast([128, 2, 64]), op=ALU.mult),
                        R=[(den, 1)], W=[(OSB, i)], X=[PB[ob]])

                prev = None
                for j in range(4):
                    for w in range(2):
                        pt = qk(j, w)
                        if prev is not None:
                            pv(*prev)
                        prev = (j, w, pt)
                pv(*prev)

            for qb in range(4):
                qblock(qb)

        for g in range(NG):
            group(g)
        S.phase_switch()
        A.reg_top = reg_mark
        if DBG >= 4:
            P3(sq, l, OSB)

    def P2B(sq, l):
        kind, jb = layer_kind(l)
        Sq = sq["S"]
        NT = Sq // 128
        NG = Sq // 512
        NP = NT // 2
        A.reset_region()
        S.phase_switch()
        OSB = A.reg("osb", [NT, D], BF16)
        reg_mark = A.reg_top
        KH = [A.reg(f"kh{i}", [Sq], BF16) for i in range(2)]
        VH = [A.reg(f"vh{i}", [NT, 129], BF16) for i in range(2)]
        QP = [[A.reg(f"qp{c}_{i}", [512], BF16) for i in range(2)] for c in range(2)]
        PT = [A.reg(f"pt{i}", [3, 512], BF16) for i in range(3)]
        O1S = A.reg("o1s", [4, 128], F32)
        OS = A.reg("os", [4, 128], F32)
        RC = [A.reg(f"rc{i}", [8], F32) for i in range(4)]
        dma("sp", WO.ap, WOB[jb].rearrange("(kc p) n -> p kc n", p=128), [b_WOB[jb]], [WO])
        load_post(GPM, g_mix_post, l)
        scp = [(0, 1, 2), (3, 4, 5)]
        oacc = [(6, 7), (6, 7)]
        steps = []
        t_ = 0
        while t_ < NT:
            n_ = 2 if (NT - t_) in (2, 4) else min(3, NT - t_)
            steps.append((t_, n_))
            t_ += n_
        pipe = []
        LOOK = 2

        def push(qk_fn, pv_fn):
            pt = qk_fn()
            pipe.append((pv_fn, pt))
            if len(pipe) > LOOK:
                f, a = pipe.pop(0)
                f(a)

        def flush():
            while pipe:
                f, a = pipe.pop(0)
                f(a)

        for c_ in range(2):
            for qp_ in QP[c_]:
                S.add("dve", lambda e, qp_=qp_: e.memset(qp_.ap, 0.0), W=[qp_])

        def ov(c):
            return ps[:, oacc[c][0]:oacc[c][1] + 1, 0:258].rearrange("p b (q c) -> p b q c", c=129)

        def head(h):
            kh, vh = KH[h % 2], VH[h % 2]
            dma("sp", kh.ap, KT[h][:, 0:Sq], [b_KT[h]], [kh])
            dma("sp", vh.ap, VB[h][:, 0:NT, :], [b_VB], [vh])

            def qgroup(qg):
                qi = nxt("qh") % 2
                qps = [QP[0][qi], QP[1][qi]]
                for c_ in range(2):
                    dma("sp", qps[c_].ap[64 * c_:64 * c_ + 64, :], QT[h][64 * c_:64 * c_ + 64, qg * 512:(qg + 1) * 512],
                        [(b_QT[h], qg)], [(qps[c_], "d")])
                qh = qps[0]
                if qg == 0 and NWARM:
                    def warm(e):
                        for i_ in range(NWARM):
                            ins = e.matmul(ps[:, i_ % 2, :], lhsT=IDB, rhs=HT[0].ap[:, 0, :], start=True, stop=True)
                        return ins
                    S.add("pe", warm, R=[CSTB, HT[0], kh, vh, qh], W=[PB[0], PB[1]])

                def cmap(c):
                    ob = oacc[c]

                    def qk(t0, nt, c=c):
                        sc = scp[nxt("scb") % 2]
                        pt = PT[nxt("ptb") % 3]

                        def mm(e):
                            for kk in range(nt):
                                kt = t0 + kk
                                ins = e.matmul(ps[:, sc[kk], :], lhsT=kh.ap[:, kt * 128:(kt + 1) * 128],
                                               rhs=qps[c].ap, start=True, stop=True)
                            return ins
                        pbs_ = [PB[sc[kk]] for kk in range(nt)]
                        S.add("pe", mm, R=[kh, qps[c]], W=pbs_)
                        S.add("act", lambda e: e.activation(out=pt.ap[:, 0:nt, :], in_=ps[:, sc[0]:sc[0] + nt, :],
                                                            func=AF.Exp, scale=0.125), W=[pt], X=pbs_)
                        return pt

                    def pv(t0, nt, pt, ob=ob):
                        def mm(e):
                            for kk in range(nt):
                                kt = t0 + kk
                                for qt in range(4):
                                    ins = e.matmul(ps[:, ob[qt // 2], (qt % 2) * 129:(qt % 2) * 129 + 129],
                                                   lhsT=pt.ap[:, kk, qt * 128:(qt + 1) * 128], rhs=vh.ap[:, kt, :],
                                                   start=(kt == 0 and qt % 2 == 0),
                                                   stop=(kt == NT - 1), skip_group_check=True)
                            return ins
                        S.add("pe", mm, R=[pt, vh], W=[PB[ob[0]], PB[ob[1]]])

                    def step_pv(t0, nt, pt):
                        pv(t0, nt, pt)
                        if t0 + nt == NT:
                            round_end_of(c, ob)

                    for (t0, nt) in steps:
                        push(lambda t0=t0, nt=nt: qk(t0, nt), lambda pt, t0=t0, nt=nt: step_pv(t0, nt, pt))

                def round_end_of(c, ob):
                    rc = RC[nxt("rc") % 4]
                    pbs = [PB[ob[0]], PB[ob[1]]]
                    rcv = rc.ap[:, 0:4].rearrange("p (b q) -> p b q", b=2)
                    if c == 0:
                        S.add("dve", lambda e, rcv=rcv: e.reciprocal(out=rcv, in_=ov(0)[:, :, :, 128]),
                              W=[(rc, 0)], X=pbs)
                        S.add("dve", lambda e, rcv=rcv: e.tensor_tensor(
                            out=O1S.ap.rearrange("p (b q) c -> p b q c", b=2), in0=ov(0)[:, :, :, 0:128],
                            in1=rcv.unsqueeze(3).to_broadcast([128, 2, 2, 128]), op=ALU.mult),
                            R=[(rc, 0)], W=[O1S], X=pbs)
                    else:
                        S.add("dve", lambda e, rcv=rcv: e.reciprocal(out=rcv, in_=ov(1)[:, :, :, 128]),
                              W=[(rc, 0)], X=pbs)
                        S.add("dve", lambda e, rc=rc: e.tensor_scalar(
                            out=rc.ap[:, 4:8], in0=rc.ap[:, 0:4], scalar1=NEGLAM.ap[:, jb:jb + 1], scalar2=None,
                            op0=ALU.mult), R=[(rc, 0), NEGLAM], W=[(rc, 1)])
                        S.add("dve", lambda e, rc=rc: e.tensor_tensor(
                            out=OS.ap.rearrange("p (b q) c -> p b q c", b=2), in0=ov(1)[:, :, :, 0:128],
                            in1=rc.ap[:, 4:8].rearrange("p (b q) -> p b q", b=2).unsqueeze(3).to_broadcast(
                                [128, 2, 2, 128]), op=ALU.mult), R=[(rc, 1)], W=[OS], X=pbs)
                        S.add("dve", lambda e: e.tensor_tensor(out=OS.ap, in0=OS.ap, in1=O1S.ap, op=ALU.add),
                              R=[O1S], W=[OS])
                        st = STAT[nxt("st") % 16]
                        st2 = STAT[nxt("st") % 16]
                        for qt in range(4):
                            jk = JK[nxt("jk") % 3]
                            S.add("act", lambda e, qt=qt, st=st, jk=jk: e.activation(
                                out=jk.ap[:, 0:128], in_=OS.ap[:, qt, :], func=AF.Square,
                                accum_out=st.ap[:, qt:qt + 1]), R=[OS], W=[(st, qt), jk])
                        S.add("act", lambda e, st=st: e.activation(out=st.ap[:, 4:8], in_=st.ap[:, 0:4], func=AF.Ln,
                                                                   scale=1.0 / 128, bias=EPSB.ap[:, 0:1]),
                              R=[(st, q_) for q_ in range(4)] + [EPSB], W=[(st, 4)])
                        S.add("act", lambda e, st=st, st2=st2: e.activation(out=st2.ap[:, 0:4], in_=st.ap[:, 4:8],
                                                                            func=AF.Exp, scale=-0.5),
                              R=[(st, 4)], W=[st2])
                        for qt in range(4):
                            tile = 4 * qg + qt
                            S.add("dve", lambda e, qt=qt, tile=tile, st2=st2: e.scalar_tensor_tensor(
                                out=OSB.ap[:, tile, h * 128:(h + 1) * 128], in0=OS.ap[:, qt, :],
                                scalar=st2.ap[:, qt:qt + 1], in1=SUBG.ap[:, jb, :], op0=ALU.mult, op1=ALU.mult),
                                R=[OS, st2, SUBG], W=[(OSB, tile)])

                for c in range(2):
                    cmap(c)

            for qg in range(NG):
                qgroup(qg)

        for h in range(8):
            head(h)
        flush()
        S.phase_switch()
        A.reg_top = reg_mark
        P3(sq, l, OSB)

    def P4(sq, l):
        Sq = sq["S"]
        NG = Sq // 512
        A.reset_region()
        S.phase_switch()
        WD = A.reg("wd", [NFF, D], BF16)
        AT = A.reg("at", [NFF, 512], BF16)
        SG = [A.reg(f"sg{i}", [512], F32) for i in range(2)]
        XNS = [A.reg(f"xn{i}", [D], BF16) for i in range(8)]

        def xset(g):
            return XNS[(g % 2) * 4:(g % 2) * 4 + 4]
        front_a(sq, l, 0, True, xset(0))
        dma("sp", WD.ap, WDN[l].rearrange("(kc p) n -> p kc n", p=128), [b_WDN[l]], [WD])
        load_post(GPF, g_ffn_post, l)
        gub = [(2, 3), (4, 5)]
        dnb = [(6, 7), (4, 5)]

        def gate_up(g):
            ht = HT[g % 2]
            for j in range(NFF):
                ws = WS[nxt("ws") % 4]
                dma("sp", ws.ap, WGU[l][j], [b_WGU[l]], [ws])
                bk = gub[nxt("gub") % 2]
                sg = SG[nxt("sg") % 2]

                def mm(e, ws=ws, bk=bk):
                    for hf in range(2):
                        for kc in range(KC):
                            i = e.matmul(ps[:, bk[hf], :], lhsT=ws.ap[:, kc, hf * 128:(hf + 1) * 128],
                                         rhs=ht.ap[:, kc, :], start=(kc == 0), stop=(kc == KC - 1))
                    return i
                S.add("pe", mm, R=[ws, ht], W=[PB[bk[0]], PB[bk[1]]])
                S.add("act", lambda e, sg=sg, bk=bk: e.activation(out=sg.ap, in_=ps[:, bk[0], :], func=AF.Silu),
                      W=[sg], X=[PB[bk[0]]])
                S.add("dve", lambda e, sg=sg, bk=bk, j=j: e.tensor_tensor(out=AT.ap[:, j, :], in0=sg.ap,
                                                                          in1=ps[:, bk[1], :], op=ALU.mult),
                      R=[sg], W=[(AT, j)], X=[PB[bk[1]]])

        def down(g):
            for t in range(4):
                tile = 4 * g + t
                bk = dnb[nxt("dnb") % 2]

                def mm(e, t=t, bk=bk):
                    for hf in range(2):
                        for kc in range(NFF):
                            i = e.matmul(ps[:, bk[hf], :], lhsT=AT.ap[:, kc, t * 128:(t + 1) * 128],
                                         rhs=WD.ap[:, kc, hf * 512:(hf + 1) * 512],
                                         start=(kc == 0), stop=(kc == NFF - 1))
                    return i
                S.add("pe", mm, R=[AT, WD], W=[PB[bk[0]], PB[bk[1]]])
                post_tile(sq, l, tile, bk, GPF, True)

        front_b(l, 1, 0, xset(0))
        if NG > 1:
            front_a(sq, l, 1, True, xset(1))
        for g in range(NG):
            gate_up(g)
            if g + 1 < NG:
                front_b(l, 1, g + 1, xset(g + 1))
            if g + 2 < NG:
                front_a(sq, l, g + 2, True, xset(g + 2))
            down(g)

    stop = cfg.get("stop", 99)
    prep()
    nph = 0
    for sq in seqs:
        for l in range(DEPTH):
            for ph in (P1, P2A if l % 2 == 0 else P2B, P4):
                nph += 1
                if nph <= stop:
                    ph(sq, l)
    S.add("sp", None, R=[b for sq in seqs for b in sq["yb"]])
    S.finalize()

    sems = {}
    for e_ in ("pe", "act", "dve", "pool"):
        sems[e_] = nc.alloc_semaphore(name="s_" + e_)
    for q_ in ("sp", "pool"):
        for i in range(NDMA_SEM):
            sems[(q_, i)] = nc.alloc_semaphore(name=f"d_{q_}{i}")
    nw = {}
    with nc.Block() as block:
        @block.sync
        def _(e):
            nw["sp"] = S.emit("sp", e, sems)

        @block.tensor
        def _(e):
            nw["pe"] = S.emit("pe", e, sems)

        @block.scalar
        def _(e):
            nw["act"] = S.emit("act", e, sems)

        @block.vector
        def _(e):
            nw["dve"] = S.emit("dve", e, sems)

        @block.gpsimd
        def _(e):
            nw["pool"] = S.emit("pool", e, sems)
    if cfg.get("verbose"):
        print("ops", {e: len(S.q[e]) for e in ENGS}, "waits", nw, "arena top", A.shared_top)
    return nc


def make_consts(smax):
    inv = (1.0 / (10000.0 ** (np.arange(0, 64, 2, dtype=np.float32) / np.float32(64)))).astype(np.float32)
    ang = np.arange(smax, dtype=np.float32)[:, None] * inv[None, :]
    cos = np.cos(ang).astype(np.float32)
    sin = np.sin(ang).astype(np.float32)
    idx = (np.arange(128) % 64) % 32
    cos_t = np.ascontiguousarray(cos[:, idx].T)
    sin_t = np.ascontiguousarray(sin[:, idx].T)
    cst = np.zeros((128, 8, 128), np.float32)
    cst[:, 0, :] = np.eye(128, dtype=np.float32)
    for m in range(128):
        if (m % 64) < 32:
            cst[m + 32, 1, m] = -1.0
        else:
            cst[m - 32, 1, m] = 1.0
    k = np.arange(128)[:, None]
    q = np.arange(128)[None, :]
    cst[:, 2, :] = (k >= q).astype(np.float32)
    cst[:, 3, :] = (k <= q).astype(np.float32)
    cst[:, 4, :] = cst[:, 5, :] = (cst[:, 2, :] - 1.0) * 30000.0
    cst[:, 6, :] = cst[:, 7, :] = (cst[:, 3, :] - 1.0) * 30000.0
    return cos_t, sin_t, cst


W_NAMES = ["norm_mix_pre", "norm_mix_post", "norm_ffn_pre", "norm_ffn_post", "a_w_in", "a_w_o", "a_sinks",
           "b_w_in", "b_w_o", "b_lambda_q1", "b_lambda_k1", "b_lambda_q2", "b_lambda_k2", "b_subln",
           "ffn_w_gate_up", "ffn_w_down"]

_CACHE = {}


def run(inputs, n_cores=8, verbose=False, trace=False, stop=99):
    xp = np.asarray(inputs["x_prompt"], np.float32)
    xs = np.asarray(inputs["x_sample"], np.float32)
    depth = inputs["norm_mix_pre"].shape[0]
    npc, nsc = xp.shape[0] // n_cores, xs.shape[0] // n_cores
    cfg = dict(depth=depth, seqs=[(xp.shape[1], npc), (xs.shape[1], nsc)], verbose=verbose, stop=stop)
    key = (depth, xp.shape, xs.shape, stop)
    if key not in _CACHE:
        _CACHE[key] = build(cfg)
    nc = _CACHE[key]
    smax = max(xp.shape[1], xs.shape[1])
    cos_t, sin_t, cst = make_consts(smax)
    shared = {k: np.ascontiguousarray(np.asarray(inputs[k], np.float32)) for k in W_NAMES
              if not (depth < 2 and k.startswith("b_"))}
    shared.update(cos_t=cos_t, sin_t=sin_t, cst=cst)
    in_maps = []
    for c in range(n_cores):
        m = dict(shared)
        m["x0"] = np.ascontiguousarray(xp[c * npc:(c + 1) * npc])
        m["x1"] = np.ascontiguousarray(xs[c * nsc:(c + 1) * nsc])
        in_maps.append(m)
    res = run_bass_kernel_spmd(nc, in_maps, core_ids=list(range(n_cores)), trace=trace)
    yp = np.concatenate([res.results[c]["y0"] for c in range(n_cores)], axis=0).astype(np.float32)
    ys = np.concatenate([res.results[c]["y1"] for c in range(n_cores)], axis=0).astype(np.float32)
    return (yp, ys), res


def kernel(**inputs):
    out, _ = run(inputs)
    return out
```
